# Optimizing a Trainium2 kernel written in Bass

```python
import math
import jax, jax.numpy as jnp
from jax import lax
import numpy as np

D_MODEL = 1024
BATCH = 16
SEQ = 2048
DEPTH = 1

HEAD_DIM = 64
DIFF_HEADS = 4
DIFF_VDIM = 2 * HEAD_DIM
DIL_HEADS = 8
DIL_CONFIGS = ((128, 1), (512, 4), (2048, 16))
DIFF_WIDTH = DIFF_HEADS * DIFF_VDIM
DIL_WIDTH = DIL_HEADS * HEAD_DIM
MIX_WIDTH = DIFF_WIDTH + DIL_WIDTH
DIFF_QK_COLS = DIFF_HEADS * 2 * HEAD_DIM
IN_COLS = 2 * DIFF_QK_COLS + DIFF_WIDTH + 3 * DIL_WIDTH
D_FF = 2816
ROPE_THETA = 500000.0
ROPE_DIM = HEAD_DIM // 4
BLOCK = 128
EPS = 1e-5

kernel_name = "hymba_diff_dilated_macaron"


def rmsnorm(x, g):
    xf = x.astype(jnp.float32)
    y = xf * lax.rsqrt(jnp.mean(xf * xf, axis=-1, keepdims=True) + EPS)
    return (y * g.astype(jnp.float32)).astype(x.dtype)


def swiglu(x, w_gate, w_up, w_down):
    return (jax.nn.silu(x @ w_gate) * (x @ w_up)) @ w_down


def rope_partial(x, positions):
    half = ROPE_DIM // 2
    inv = jnp.exp(-math.log(ROPE_THETA) * jnp.arange(half, dtype=jnp.float32) * 2.0 / ROPE_DIM)
    ang = positions.astype(jnp.float32)[:, :, None, None] * inv
    cos, sin = jnp.cos(ang), jnp.sin(ang)
    xf = x.astype(jnp.float32)
    x1, x2 = xf[..., :half], xf[..., half:ROPE_DIM]
    out = jnp.concatenate([x1 * cos - x2 * sin, x2 * cos + x1 * sin, xf[..., ROPE_DIM:]], axis=-1)
    return out.astype(x.dtype)


def diff_attention(q, k, v, lam):
    B, S = q.shape[0], q.shape[1]
    nb = S // BLOCK
    scale = HEAD_DIM ** -0.5
    qb = q.reshape(B, nb, BLOCK, *q.shape[2:]).transpose(1, 0, 2, 3, 4, 5)
    kpos = jnp.arange(S)

    def one_block(args):
        qblk, bi = args
        s = jnp.einsum('bqhcd,bkhcd->bhcqk', qblk, k).astype(jnp.float32) * scale
        qpos = bi * BLOCK + jnp.arange(BLOCK)
        mask = kpos[None, :] <= qpos[:, None]
        p = jax.nn.softmax(jnp.where(mask, s, -jnp.inf), axis=-1)
        attn = p[:, :, 0] - lam * p[:, :, 1]
        return jnp.einsum('bhqk,bkhe->bqhe', attn.astype(v.dtype), v)

    out = lax.map(one_block, (qb, jnp.arange(nb)))
    return out.transpose(1, 0, 2, 3, 4).reshape(B, S, q.shape[2], v.shape[-1])


def dilated_branch(q, k, v, window, dilation):
    B, S, H, dh = q.shape
    L = S // dilation
    w = window // dilation
    Lp = -(-L // BLOCK) * BLOCK
    nb = Lp // BLOCK
    n_prev = -(-w // BLOCK)
    KB = (n_prev + 1) * BLOCK
    scale = dh ** -0.5

    def to_sub(t):
        return t.reshape(B, L, dilation, H, dh).transpose(0, 2, 3, 1, 4)

    qs = jnp.pad(to_sub(q), ((0, 0), (0, 0), (0, 0), (0, Lp - L), (0, 0)))
    qb = qs.reshape(B, dilation, H, nb, BLOCK, dh)

    def band(t):
        tp = jnp.pad(to_sub(t), ((0, 0), (0, 0), (0, 0), (n_prev * BLOCK, Lp - L), (0, 0)))
        tb = tp.reshape(B, dilation, H, nb + n_prev, BLOCK, dh)
        return jnp.concatenate([tb[:, :, :, j:j + nb] for j in range(n_prev + 1)], axis=4)

    kb, vb = band(k), band(v)
    s = jnp.einsum('brhnqd,brhnkd->brhnqk', qb, kb).astype(jnp.float32) * scale
    q_off = jnp.arange(BLOCK)[:, None]
    k_off = jnp.arange(KB)[None, :]
    dist = q_off + n_prev * BLOCK - k_off
    kpos = jnp.arange(nb)[:, None, None] * BLOCK - n_prev * BLOCK + k_off[None]
    valid = (dist >= 0) & (dist <= w) & (kpos >= 0)
    s = jnp.where(valid, s, -jnp.inf)
    m = jnp.max(s, axis=-1, keepdims=True)
    p = jnp.exp(s - m)
    denom = jnp.sum(p, axis=-1)
    o = jnp.einsum('brhnqk,brhnkd->brhnqd', p.astype(v.dtype), vb).astype(jnp.float32) / denom[..., None]
    lse = m[..., 0] + jnp.log(denom)
    o = o.reshape(B, dilation, H, Lp, dh)[:, :, :, :L].transpose(0, 3, 1, 2, 4).reshape(B, S, H, dh)
    lse = lse.reshape(B, dilation, H, Lp)[..., :L].transpose(0, 3, 1, 2).reshape(B, S, H)
    return o, lse


def dilated_attention(q, k, v):
    outs, lses = [], []
    for window, dilation in DIL_CONFIGS:
        o, l = dilated_branch(q, k, v, window, dilation)
        outs.append(o)
        lses.append(l)
    wts = jax.nn.softmax(jnp.stack(lses, axis=0), axis=0)
    out = jnp.sum(wts[..., None] * jnp.stack(outs, axis=0), axis=0)
    return out.astype(q.dtype)


def setup_inputs(seed: int = 0) -> dict:
    key = jax.random.key(seed)
    ks = jax.random.split(key, 20)
    f32 = jnp.float32

    def nrm(k, shape, scale):
        return jax.random.normal(k, shape, f32) * scale

    def gain(k, shape):
        return 1.0 + 0.02 * jax.random.normal(k, shape, f32)

    return {
        "x": jax.random.normal(ks[0], (BATCH, SEQ, D_MODEL), f32),
        "positions": jnp.broadcast_to(jnp.arange(SEQ, dtype=jnp.int32), (BATCH, SEQ)),
        "ffn1_norm": gain(ks[1], (DEPTH, D_MODEL)),
        "ffn1_gate": nrm(ks[2], (DEPTH, D_MODEL, D_FF), D_MODEL ** -0.5),
        "ffn1_up": nrm(ks[3], (DEPTH, D_MODEL, D_FF), D_MODEL ** -0.5),
        "ffn1_down": nrm(ks[4], (DEPTH, D_FF, D_MODEL), D_FF ** -0.5),
        "mix_norm": gain(ks[5], (DEPTH, D_MODEL)),
        "w_in": nrm(ks[6], (DEPTH, D_MODEL, IN_COLS), D_MODEL ** -0.5),
        "lambda_q1": nrm(ks[7], (DEPTH, HEAD_DIM), 0.1),
        "lambda_k1": nrm(ks[8], (DEPTH, HEAD_DIM), 0.1),
        "lambda_q2": nrm(ks[9], (DEPTH, HEAD_DIM), 0.1),
        "lambda_k2": nrm(ks[10], (DEPTH, HEAD_DIM), 0.1),
        "subln_gain": gain(ks[11], (DEPTH, DIFF_VDIM)),
        "w_out": nrm(ks[12], (DEPTH, MIX_WIDTH, D_MODEL), MIX_WIDTH ** -0.5),
        "ffn2_norm": gain(ks[13], (DEPTH, D_MODEL)),
        "ffn2_gate": nrm(ks[14], (DEPTH, D_MODEL, D_FF), D_MODEL ** -0.5),
        "ffn2_up": nrm(ks[15], (DEPTH, D_MODEL, D_FF), D_MODEL ** -0.5),
        "ffn2_down": nrm(ks[16], (DEPTH, D_FF, D_MODEL), D_FF ** -0.5),
        "final_norm": gain(ks[17], (D_MODEL,)),
    }


def reference(x, positions, ffn1_norm, ffn1_gate, ffn1_up, ffn1_down, mix_norm, w_in,
              lambda_q1, lambda_k1, lambda_q2, lambda_k2, subln_gain, w_out,
              ffn2_norm, ffn2_gate, ffn2_up, ffn2_down, final_norm):
    B, S, _ = x.shape
    for l in range(DEPTH):
        x = x + 0.5 * swiglu(rmsnorm(x, ffn1_norm[l]), ffn1_gate[l], ffn1_up[l], ffn1_down[l])

        h = rmsnorm(x, mix_norm[l])
        proj = h @ w_in[l]
        splits = np.cumsum([DIFF_QK_COLS, DIFF_QK_COLS, DIFF_WIDTH, DIL_WIDTH, DIL_WIDTH]).tolist()
        dq, dk, dv, gq, gk, gv = jnp.split(proj, splits, axis=-1)

        lambda_init = 0.8 - 0.6 * math.exp(-0.3 * l)
        lam = (jnp.exp(jnp.sum(lambda_q1[l].astype(jnp.float32) * lambda_k1[l].astype(jnp.float32)))
               - jnp.exp(jnp.sum(lambda_q2[l].astype(jnp.float32) * lambda_k2[l].astype(jnp.float32)))
               + lambda_init)
        dq = rope_partial(dq.reshape(B, S, DIFF_HEADS * 2, HEAD_DIM), positions).reshape(B, S, DIFF_HEADS, 2, HEAD_DIM)
        dk = rope_partial(dk.reshape(B, S, DIFF_HEADS * 2, HEAD_DIM), positions).reshape(B, S, DIFF_HEADS, 2, HEAD_DIM)
        dv = dv.reshape(B, S, DIFF_HEADS, DIFF_VDIM)
        a_out = diff_attention(dq, dk, dv, lam)
        a_out = rmsnorm(a_out, subln_gain[l]) * (1.0 - lambda_init)
        a_out = a_out.reshape(B, S, DIFF_WIDTH)

        gq = rope_partial(gq.reshape(B, S, DIL_HEADS, HEAD_DIM), positions)
        gk = rope_partial(gk.reshape(B, S, DIL_HEADS, HEAD_DIM), positions)
        gv = gv.reshape(B, S, DIL_HEADS, HEAD_DIM)
        b_out = dilated_attention(gq, gk, gv).reshape(B, S, DIL_WIDTH)

        x = x + jnp.concatenate([a_out, b_out], axis=-1) @ w_out[l]

        x = x + 0.5 * swiglu(rmsnorm(x, ffn2_norm[l]), ffn2_gate[l], ffn2_up[l], ffn2_down[l])
    return rmsnorm(x, final_norm)
```

```python
import math
import numpy as np
import ml_dtypes
import concourse.bass as bass
import concourse.mybir as mybir
from concourse.bass_utils import run_bass_kernel_spmd

F32 = mybir.dt.float32
BF16 = mybir.dt.bfloat16
I32 = mybir.dt.int32
ALU = mybir.AluOpType
AF = mybir.ActivationFunctionType

NCORES = 8
D = 1024
S = 2048
DFF = 2816
NCH = DFF // 128
NT = S // 128
EPS = 1e-5
G = 4
GROUP_END = {2: 0, 5: 3, 9: 6, 13: 10, 17: 14, 21: 18}
LAMBDA_INIT = 0.8 - 0.6 * math.exp(-0.3 * 0)


class Res:
    __slots__ = ("name", "last_w", "readers", "excl", "gen", "dgen")

    def __init__(self, name, excl=False):
        self.name = name
        self.last_w = None
        self.readers = []
        self.excl = excl
        self.gen = 0
        self.dgen = 0


class Op:
    __slots__ = ("eng", "fn", "deps", "dma_key", "need_inc", "signal", "idx")

    def __init__(self, eng, fn, dma_key):
        self.eng = eng
        self.fn = fn
        self.dma_key = dma_key
        self.deps = []
        self.need_inc = dma_key is not None
        self.signal = None


class Sched:
    ENGS = ("pe", "act", "dve", "pool", "sp")

    def __init__(self):
        self.ops = {e: [] for e in self.ENGS}
        self.dma_keys = {}
        self.final_deps = []
        self.pending = {}
        self.n_ops = 0
        self.verify = None
        self.base_gen = {}
        self.cur_rec = None
        self.dry = False
        self.expect = None
        self.expect_pos = 0

    def barrier(self):
        lasts = [self.ops[e][-1] for e in self.ENGS if self.ops[e]]
        dl = {}
        for e in self.ENGS:
            for o in self.ops[e]:
                if o.dma_key is not None and (o.dma_key not in dl or dl[o.dma_key].idx < o.idx):
                    dl[o.dma_key] = o
        allp = lasts + list(dl.values())
        self.pending = {e: list(allp) for e in self.ENGS}

    def op(self, eng, fn, reads=(), writes=(), dma_key=None):
        if self.dry:
            rec = []
            for r in list(reads) + list(writes):
                if r.dgen == 0 and r not in self._touched:
                    self._touched.append(r)
            for r in reads:
                rec.append((r, r.dgen))
            for w in writes:
                w.dgen += 1
            self.cur_rec.append(rec)
            return None
        if self.verify is not None:
            rec = self.verify.pop(0)
            for (r, g) in rec:
                if r.gen - self.base_gen[id(r)] != g:
                    raise AssertionError(
                        f"PIPELINE HAZARD on {r.name}: read expects write #{g}, emission order gives #{r.gen - self.base_gen[id(r)]}")
        for w in writes:
            w.gen += 1
        o = Op(eng, fn, dma_key)
        deps = {}

        def add(d, raw):
            if d is None:
                return
            if d.dma_key is None and o.dma_key is None and d.eng == eng:
                if eng == "pe":
                    return
            deps[id(d)] = d

        for r in reads:
            if r.excl:
                continue
            add(r.last_w, True)
        wl = list(writes) + [r for r in reads if r.excl]
        for w in wl:
            add(w.last_w, True)
            for rd in w.readers:
                add(rd, False)
        if self.pending.get(eng):
            for d in self.pending[eng]:
                if d.dma_key is None and d.eng == eng:
                    continue
                deps[id(d)] = d
            self.pending[eng] = []
        o.deps = list(deps.values())
        for d in o.deps:
            d.need_inc = True
        for r in reads:
            if not r.excl:
                r.readers.append(o)
        for w in wl:
            w.last_w = o
            w.readers = []
        if dma_key is not None and dma_key not in self.dma_keys:
            self.dma_keys[dma_key] = None
        o.idx = self.n_ops
        self.n_ops += 1
        self.ops[eng].append(o)
        return o

    def emit(self, nc, block_engines, sems_eng, sems_dma):
        for e in self.ENGS:
            cnt = 0
            for o in self.ops[e]:
                if o.dma_key is None and o.need_inc:
                    cnt += 1
                    o.signal = (sems_eng[e], cnt)
        dcnt = {k: 0 for k in sems_dma}
        dmas = [o for e in self.ENGS for o in self.ops[e] if o.dma_key is not None]
        for o in sorted(dmas, key=lambda o: o.idx):
            dcnt[o.dma_key] += 16
            o.signal = (sems_dma[o.dma_key], dcnt[o.dma_key])
        return dcnt

    def run_engine(self, e, eng, final_waits=None):
        waited = {}
        for o in self.ops[e]:
            for d in o.deps:
                sem, val = d.signal
                k = id(sem)
                if waited.get(k, 0) < val:
                    eng.wait_ge(sem, val)
                    waited[k] = val
            ins = o.fn(eng)
            if o.need_inc:
                sem, _ = o.signal
                ins.then_inc(sem, 16 if o.dma_key is not None else 1)
        if final_waits:
            for sem, val in final_waits:
                if waited.get(id(sem), 0) < val:
                    eng.wait_ge(sem, val)


def build_nc(nseq=2, stages=("ffn1", "attn", "ffn2"), debug_out=None):
    nc = bass.Bass("TRN2", target_bir_lowering=False)
    SC = Sched()

    def din(name, shape, dt=F32):
        return nc.dram_tensor(name, list(shape), dt, kind="ExternalInput").ap()

    x_in = din("x", [nseq, S, D])
    pos_in = din("positions", [nseq, S], I32)
    g_ffn1 = din("ffn1_norm", [1, D])
    w1g = din("ffn1_gate", [D, DFF])
    w1u = din("ffn1_up", [D, DFF])
    w1d = din("ffn1_down", [DFF, D])
    g_mix = din("mix_norm", [1, D])
    w_in = din("w_in", [D, 3072])
    lq1 = din("lambda_q1", [1, 64])
    lk1 = din("lambda_k1", [1, 64])
    lq2 = din("lambda_q2", [1, 64])
    lk2 = din("lambda_k2", [1, 64])
    subln = din("subln_gain", [1, 128])
    w_out = din("w_out", [D, D])
    g_ffn2 = din("ffn2_norm", [1, D])
    w2g = din("ffn2_gate", [D, DFF])
    w2u = din("ffn2_up", [D, DFF])
    w2d = din("ffn2_down", [DFF, D])
    g_fin = din("final_norm", [1, D])
    c_ident = din("c_ident", [128, 128], BF16)
    c_perm = din("c_perm", [128, 128], BF16)
    c_mask = din("c_mask", [128, 256], BF16)
    c_invf = din("c_invf", [128, 1], F32)
    y_out = nc.dram_tensor("y", [nseq, S, D], F32, kind="ExternalOutput").ap()

    from contextlib import ExitStack
    es = ExitStack()

    def sb(name, shape, dt):
        return es.enter_context(nc.sbuf_tensor(name, list(shape), dt))

    def psum(name, shape, dt):
        return es.enter_context(nc.psum_tensor(name, list(shape), dt))

    x_sb = sb("x_sb", [128, NT, D], F32)
    hT = sb("hT", [128, 8, S], BF16)
    hb = sb("hb", [128, 2, D], BF16)
    junk = sb("junk", [128, D], BF16)
    ss = sb("ss", [128, NT], F32)
    rstd = sb("rstd", [128, NT], F32)
    ident = sb("ident", [128, 128], BF16)
    perm = sb("perm", [128, 128], BF16)
    maskb = sb("maskb", [128, 256], BF16)
    onesb = sb("onesb", [128, 128], BF16)
    invf = sb("invf", [128, 1], F32)
    epsc = sb("epsc", [128, 1], F32)
    lamt = sb("lamt", [128, 4, 64], F32)
    lams = sb("lams", [128, 4], F32)
    neglam = sb("neglam", [128, 1], F32)
    gcol = sb("gcol", [128, 1], F32)
    acc = sb("acc", [128, 2, S], F32)
    tmpf = sb("tmpf", [128, 6, 512], F32)
    U16 = sb("U16", [128, 37888], BF16)
    gain = tmpf[:, 4:6, :].rearrange("p a b -> p (a b)")
    sg = tmpf[:, 0:2, :]
    ostage = acc[:].rearrange("p s (a b) -> p (s a) b", a=2)
    posi = acc[:, 1, 0:1024].bitcast(I32).rearrange("p (a b) -> p a b", a=2)
    qraw = junk[:].rearrange("p (a b) -> p a b", a=2)
    wg_r = U16[:, 0:3072].rearrange("p (s c f) -> p s c f", s=3, c=8)
    wu_r = U16[:, 3072:6144].rearrange("p (s c f) -> p s c f", s=3, c=8)
    wd_r = U16[:, 6144:12288].rearrange("p (s f) -> p s f", s=6)
    aT = U16[:, 12288:28672].rearrange("p (s t) -> p s t", s=8)
    mixT = U16[:, 0:8192].rearrange("p (s t) -> p s t", s=4)
    QK = U16[:, 8192:16384].rearrange("p (s t) -> p s t", s=4)
    VV = U16[:, 16384:20480]
    Vd = VV.rearrange("p (b c) -> p b c", b=16)
    gvT = VV.rearrange("p (s t) -> p s t", s=2)
    PT = U16[:, 20480:24576].rearrange("p (s t) -> p s t", s=8)
    win = U16[:, 24576:28672].rearrange("p (s c f) -> p s c f", s=2, c=8)
    wout = U16[:, 28672:32768].rearrange("p (s f) -> p s f", s=4)
    CS = U16[:, 32768:36864].rearrange("p (s t) -> p s t", s=2)
    PTx = U16[:, 36864:37888]
    vring = U16[:, 22528:24448].rearrange("p (s f) -> p s f", s=10)

    pall = psum("pall", [128, 8, 512], F32)
    pb = [pall[:, i, :] for i in range(8)]
    pt = [pb[6].bitcast(BF16), pb[7].bitcast(BF16)]

    R_x = [Res(f"x{t}") for t in range(NT)]
    R_hT = [Res(f"hT{t}") for t in range(NT)]
    R_gain = Res("gain")
    R_hb = [Res("hb0"), Res("hb1")]
    R_junk = Res("junk")
    R_ss = [Res(f"ss{q}") for q in range(4)]
    R_rstd = [Res(f"rstd{q}") for q in range(4)]
    R_eps = Res("eps")
    R_ident = Res("ident")
    R_ostage = [Res(f"os{i}") for i in range(4)]
    R_sg = [Res("sg0"), Res("sg1")]
    R_wg = [Res(f"wg{i}") for i in range(3)]
    R_wu = [Res(f"wu{i}") for i in range(3)]
    R_wd = [Res(f"wd{i}") for i in range(6)]
    R_aT = [[Res(f"aT{i}_{t}") for t in range(4)] for i in range(8)]
    R_pb = [Res(f"pb{i}", excl=True) for i in range(8)]
    R_pt = [R_pb[6], R_pb[7]]
    R_c = Res("consts")
    R_lam = Res("lam")
    R_posi = [R_ostage[2], R_ostage[2]]
    R_tmp = [Res(f"tmp{i}") for i in range(4)]
    R_CS = [Res(f"CS{i}") for i in range(4)]
    R_qraw = [Res("qraw0"), Res("qraw1")]
    R_win = [Res("win0"), Res("win1")]
    R_QK = [[Res(f"QK{i}_{t}") for t in range(4)] for i in range(4)]
    R_VV = Res("VV")
    R_PT = [Res(f"PT{i}") for i in range(8)]
    R_PTx = Res("PTx")
    R_vr = [Res(f"vr{i}") for i in range(10)]
    R_mix = [[Res(f"mix{i}_{t}") for t in range(4)] for i in range(4)]
    R_wout = [Res(f"wout{i}") for i in range(4)]
    R_acc = [Res("accN"), Res("accD")]
    out_dmas = []

    def dma(eng, out, in_, reads, writes, key):
        return SC.op(eng, lambda e, o=out, i=in_: e.dma_start(out=o, in_=i), reads, writes, dma_key=key)

    def mm(out, lhsT, rhs, start, stop, reads, writes):
        return SC.op("pe", lambda e, o=out, l=lhsT, r=rhs, s=start, t=stop:
                     e.matmul(o, lhsT=l, rhs=r, start=s, stop=t), reads, writes)

    def act(out, in_, func, reads, writes, bias=0.0, scale=1.0, accum_out=None):
        if accum_out is None:
            return SC.op("act", lambda e, o=out, i=in_, f=func, b=bias, s=scale:
                         e.activation(out=o, in_=i, func=f, bias=b, scale=s), reads, writes)
        return SC.op("act", lambda e, o=out, i=in_, f=func, b=bias, s=scale, a=accum_out:
                     e.activation(out=o, in_=i, func=f, bias=b, scale=s, accum_out=a), reads, writes)

    def tscalar(eng, out, in0, s1, s2, op0, op1, reads, writes):
        if op1 is None:
            return SC.op(eng, lambda e, o=out, i=in0, a=s1, p=op0:
                         e.tensor_scalar(out=o, in0=i, scalar1=a, scalar2=None, op0=p), reads, writes)
        return SC.op(eng, lambda e, o=out, i=in0, a=s1, b=s2, p=op0, q=op1:
                     e.tensor_scalar(out=o, in0=i, scalar1=a, scalar2=b, op0=p, op1=q), reads, writes)

    def stt(eng, out, in0, scalar, in1, op0, op1, reads, writes):
        return SC.op(eng, lambda e, o=out, i=in0, s=scalar, j=in1, p=op0, q=op1:
                     e.scalar_tensor_tensor(out=o, in0=i, scalar=s, in1=j, op0=p, op1=q), reads, writes)

    def tt(eng, out, in0, in1, op, reads, writes):
        return SC.op(eng, lambda e, o=out, i=in0, j=in1, p=op:
                     e.tensor_tensor(out=o, in0=i, in1=j, op=p), reads, writes)

    def tcopy(eng, out, in_, reads, writes):
        return SC.op(eng, lambda e, o=out, i=in_: e.tensor_copy(out=o, in_=i), reads, writes)

    dma("sp", ident[:], c_ident, [], [R_ident], "c_ident")
    SC.op("dve", lambda e: e.memset(epsc[:], EPS), [], [R_eps])

    def load_x(s_):
        q = "sp" if s_ == 0 else "act"
        for t in range(NT):
            dma(q, x_sb[:, t, :], x_in[s_, t * 128:(t + 1) * 128, :], [], [R_x[t]], ("x", t))

    load_x(0)

    def load_constants():
        dma("sp", perm[:], c_perm, [], [R_c], "c_perm")
        dma("sp", maskb[:], c_mask, [], [R_c], "c_mask")
        dma("sp", invf[:], c_invf, [], [R_c], "c_invf")
        SC.op("dve", lambda e: e.memset(onesb[:], 1.0), [], [R_c])
        for i, lap in enumerate((lq1, lk1, lq2, lk2)):
            dma("sp", lamt[:, i, :], lap.partition_broadcast(128), [], [R_lam], ("lam", i))
        tt("dve", lamt[:, 0, :], lamt[:, 0, :], lamt[:, 1, :], ALU.mult, [R_lam], [R_lam])
        tt("dve", lamt[:, 2, :], lamt[:, 2, :], lamt[:, 3, :], ALU.mult, [R_lam], [R_lam])
        SC.op("dve", lambda e: e.reduce_sum(out=lams[:, 0:1], in_=lamt[:, 0, :], axis=mybir.AxisListType.X), [R_lam], [R_lam])
        SC.op("dve", lambda e: e.reduce_sum(out=lams[:, 1:2], in_=lamt[:, 2, :], axis=mybir.AxisListType.X), [R_lam], [R_lam])
        act(lams[:, 2:4], lams[:, 0:2], AF.Exp, [R_lam], [R_lam])
        tt("dve", lams[:, 0:1], lams[:, 3:4], lams[:, 2:3], ALU.subtract, [R_lam], [R_lam])
        tscalar("dve", neglam[:], lams[:, 0:1], -LAMBDA_INIT, None, ALU.add, None, [R_lam], [R_lam])
        dma("sp", gcol[:], subln.rearrange("o e -> e o"), [], [R_lam], "gcol")
        tscalar("dve", gcol[:], gcol[:], 1.0 - LAMBDA_INIT, None, ALU.mult, None, [R_lam], [R_lam])

    def load_gain(g_ap):
        dma("sp", gain[:], g_ap.partition_broadcast(128), [], [R_gain], "gain")

    def row_stats(q):
        sl = slice(4 * q, 4 * q + 4)
        for t in range(4 * q, 4 * q + 4):
            act(junk[:], x_sb[:, t, :], AF.Square, [R_x[t]], [R_junk, R_ss[q]], accum_out=ss[:, t:t + 1])
        act(rstd[:, sl], ss[:, sl], AF.Sqrt, [R_ss[q], R_eps], [R_rstd[q]], bias=epsc[:], scale=1.0 / D)
        SC.op("dve", lambda e: e.reciprocal(out=rstd[:, sl], in_=rstd[:, sl]), [R_rstd[q]], [R_rstd[q]])

    def norm_to_hT(g_ap):
        load_gain(g_ap)
        row_stats(0)
        for t in range(NT):
            k = t % 2
            if t % 4 == 0 and t // 4 + 1 < 4:
                row_stats(t // 4 + 1)
            stt("dve", hb[:, k, :], x_sb[:, t, :], rstd[:, t:t + 1], gain[:], ALU.mult, ALU.mult,
                [R_x[t], R_rstd[t // 4], R_gain], [R_hb[k]])
            for dc in range(8):
                SC.op("pe", lambda e, o=pt[k][:, dc * 128:(dc + 1) * 128], i=hb[:, k, dc * 128:(dc + 1) * 128]:
                      e.transpose(o, i, ident[:]), [R_hb[k], R_ident], [R_pt[k]])
            src = pt[k].rearrange("p (c t) -> p c t", c=8)
            if t % 2 == 0:
                SC.op("act", lambda e, o=hT[:, :, t * 128:(t + 1) * 128], i=src: e.copy(out=o, in_=i),
                      [R_pt[k]], [R_hT[t]])
            else:
                tcopy("dve", hT[:, :, t * 128:(t + 1) * 128], src, [R_pt[k]], [R_hT[t]])

    def ffn(wg_d, wu_d, wd_d, extra=None):
        extra = list(extra or [])
        wg_v = wg_d.rearrange("(dc p) f -> p dc f", p=128)
        wu_v = wu_d.rearrange("(dc p) f -> p dc f", p=128)
        it = 0
        yi = 0
        for c in range(NCH):
            s3, s6, s8 = c % 3, c % 6, c % 8
            dma("pool", wg_r[:, s3], wg_v[:, :, c * 128:(c + 1) * 128], [], [R_wg[s3]], ("wg", s3))
            dma("pool", wu_r[:, s3], wu_v[:, :, c * 128:(c + 1) * 128], [], [R_wu[s3]], ("wu", s3))
            dma("pool", wd_r[:, s6], wd_d[c * 128:(c + 1) * 128, :], [], [R_wd[s6]], ("wd", s6))
            for tq in range(4):
                k = it % 2
                it += 1
                gb, ub = pb[k], pb[2 + k]
                hres = R_hT[tq * 4:(tq + 1) * 4]
                for dc in range(8):
                    mm(gb[:], wg_r[:, s3, dc, :], hT[:, dc, tq * 512:(tq + 1) * 512], dc == 0, dc == 7,
                       [R_wg[s3]] + hres, [R_pb[k]])
                for dc in range(8):
                    mm(ub[:], wu_r[:, s3, dc, :], hT[:, dc, tq * 512:(tq + 1) * 512], dc == 0, dc == 7,
                       [R_wu[s3]] + hres, [R_pb[2 + k]])
                if extra and c >= 1:
                    extra.pop(0)()
                act(sg[:, k, :], gb[:], AF.Silu, [R_pb[k]], [R_sg[k]])
                tt("dve", aT[:, s8, tq * 512:(tq + 1) * 512], sg[:, k, :], ub[:], ALU.mult,
                   [R_sg[k], R_pb[2 + k]], [R_aT[s8][tq]])
            if c in GROUP_END:
                grp = list(range(GROUP_END[c], c + 1))
                for b in range(NT):
                    for dh in range(2):
                        k = yi % 2
                        yi += 1
                        yb = pb[4 + k]
                        for j, cc in enumerate(grp):
                            mm(yb[:], aT[:, cc % 8, b * 128:(b + 1) * 128], wd_r[:, cc % 6, dh * 512:(dh + 1) * 512],
                               j == 0, j == len(grp) - 1, [R_aT[cc % 8][b // 4], R_wd[cc % 6]], [R_pb[4 + k]])
                        xs = x_sb[:, b, dh * 512:(dh + 1) * 512]
                        stt("dve", xs, yb[:], 0.5, xs, ALU.mult, ALU.add, [R_pb[4 + k], R_x[b]], [R_x[b]])
        while extra:
            extra.pop(0)()


    PI = math.pi
    rope_done = [False]
    consts_done = [False]
    PIC = 3.1415925
    pe_cnt = [0, 0, 0]

    def run_pipeline(st1, st2, depth):
        n = len(st1)
        recs1, recs2, touched = [], [], []
        SC.dry, SC._touched = True, touched
        for i in range(n):
            SC.cur_rec = []
            st1[i]()
            recs1.append(SC.cur_rec)
            SC.cur_rec = []
            st2[i]()
            recs2.append(SC.cur_rec)
        SC.dry = False
        for r in touched:
            r.dgen = 0
        SC.base_gen = {id(r): r.gen for r in touched}
        for i in range(n + depth):
            if i < n:
                SC.verify = list(recs1[i])
                st1[i]()
            if i >= depth:
                SC.verify = list(recs2[i - depth])
                st2[i - depth]()
        SC.verify = None

    def rope_table_ops(s_):
        ops = []
        tv = acc[:, 0, :].rearrange("p (a b) -> p a b", a=4)
        R_t = [R_ostage[0], R_ostage[0], R_ostage[1], R_ostage[1]]
        for tq in range(4):
            k = tq % 2
            ang, ni, r_, m_ = tv[:, 0, :], tv[:, 1, :].bitcast(I32), tv[:, 2, :], tv[:, 3, :]
            ops.append(lambda tq=tq, k=k: dma("sp", posi[:, k, :],
                                              pos_in[s_:s_ + 1, tq * 512:(tq + 1) * 512].partition_broadcast(128),
                                              [], [R_posi[k]], ("posi", k)))
            ops.append(lambda k=k, ang=ang: tscalar("dve", ang, posi[:, k, :], invf[:, 0:1], None, ALU.mult, None,
                                                    [R_posi[k], R_c], [R_t[0]]))
            ops.append(lambda ang=ang, ni=ni: tscalar("dve", ni, ang, 1.0 / (2 * PI), None, ALU.mult, None, [R_t[0]], [R_t[1]]))
            ops.append(lambda ni=ni, m_=m_: tscalar("dve", m_, ni, -2 * PI, None, ALU.mult, None, [R_t[1]], [R_t[3]]))
            ops.append(lambda ang=ang, r_=r_, m_=m_: tt("dve", r_, m_, ang, ALU.add, [R_t[3], R_t[0]], [R_t[2]]))
            for which in (1, 0):
                if which == 0:
                    ops.append(lambda r_=r_: tscalar("dve", r_, r_, PI / 2, None, ALU.add, None, [R_t[2]], [R_t[2]]))
                ops.append(lambda r_=r_, m_=m_: tscalar("dve", m_, r_, PIC, -2 * PI, ALU.is_gt, ALU.mult, [R_t[2]], [R_t[3]]))
                ops.append(lambda r_=r_, m_=m_: tt("dve", r_, r_, m_, ALU.add, [R_t[2], R_t[3]], [R_t[2]]))
                ops.append(lambda r_=r_: tscalar("dve", r_, r_, -PIC, PIC, ALU.max, ALU.min, [R_t[2]], [R_t[2]]))
                ops.append(lambda r_=r_, which=which, tq=tq: act(CS[:, which, tq * 512:(tq + 1) * 512], r_, AF.Sin,
                                                                  [R_t[2]], [R_CS[tq]]))
        return ops

    def load_win(col0, ncols):
        k = pe_cnt[0] % 2
        pe_cnt[0] += 1
        win_dma(k, col0, ncols)
        return k

    w_in_v = w_in.rearrange("(dc p) f -> p dc f", p=128)

    def win_dma(wk, col0, ncols):
        dma("pool", win[:, wk, :, 0:ncols], w_in_v[:, :, col0:col0 + ncols], [], [R_win[wk]], ("win", wk))

    def proj_group(specs, st1, st2):
        slots = []
        for _ in specs:
            slots.append(pe_cnt[0] % 2)
            pe_cnt[0] += 1
        win_dma(slots[0], specs[0][0], 128)
        for j, (col0, dst, dres, rope) in enumerate(specs):
            wk = slots[j]
            for tq in range(4):
                k = pe_cnt[1] % 2
                pe_cnt[1] += 1

                def s1(j=j, tq=tq, k=k, wk=wk, dst=dst, dres=dres, rope=rope):
                    if tq == 0 and j + 1 < len(specs):
                        win_dma(slots[j + 1], specs[j + 1][0], 128)
                    A = pb[k]
                    hres = R_hT[tq * 4:(tq + 1) * 4]
                    for dc in range(8):
                        mm(A[:], win[:, wk, dc, 0:128], hT[:, dc, tq * 512:(tq + 1) * 512], dc == 0, dc == 7,
                           [R_win[wk]] + hres, [R_pb[k]])
                    if not rope:
                        SC.op("act", lambda e, o=dst[:, tq * 512:(tq + 1) * 512], i=A[:]: e.copy(out=o, in_=i),
                              [R_pb[k]], [dres[tq]])
                    else:
                        SC.op("act", lambda e, o=qraw[:, k, :], i=A[:]: e.copy(out=o, in_=i), [R_pb[k]], [R_qraw[k]])

                def s2(tq=tq, k=k, dst=dst, dres=dres, rope=rope):
                    if not rope:
                        return
                    dsl = dst[:, tq * 512:(tq + 1) * 512]
                    B = pb[2 + k]
                    mm(B[:], perm[:], qraw[:, k, :], True, True, [R_c, R_qraw[k]], [R_pb[2 + k]])
                    t1, t2 = tmpf[:, k, :], tmpf[:, 2 + k, :]
                    tt("dve", t1, qraw[:, k, :], CS[:, 0, tq * 512:(tq + 1) * 512], ALU.mult, [R_qraw[k], R_CS[tq]], [R_tmp[k]])
                    tt("dve", t2, B[:], CS[:, 1, tq * 512:(tq + 1) * 512], ALU.mult, [R_pb[2 + k], R_CS[tq]], [R_tmp[2 + k]])
                    tt("pool", dsl, t1, t2, ALU.add, [R_tmp[k], R_tmp[2 + k]], [dres[tq]])

                st1.append(s1)
                st2.append(s2)

    def wout_dma(row0):
        for j in range(4):
            dma("pool", wout[:, j, :], w_out[row0 + j * 128:row0 + (j + 1) * 128, :], [], [R_wout[j]], ("wout", j))

    def out_proj(row0, blocks=None):
        yi = 0
        for b in (range(NT) if blocks is None else blocks):
            for dh in range(2):
                k = yi % 2
                yi += 1
                yb = pb[4 + k]
                for j in range(4):
                    mm(yb[:], mixT[:, j, b * 128:(b + 1) * 128], wout[:, j, dh * 512:(dh + 1) * 512], j == 0, j == 3,
                       [R_mix[j][b // 4], R_wout[j]], [R_pb[4 + k]])
                xs = x_sb[:, b, dh * 512:(dh + 1) * 512]
                tt("dve", xs, yb[:], xs, ALU.add, [R_pb[4 + k], R_x[b]], [R_x[b]])

    def diff_stage():
        wout_dma(0)
        st_i = 0
        for g in range(2):
            pj1, pj2, specs = [], [], []
            for hl in range(2):
                h = 2 * g + hl
                specs.append((h * 128, QK[:, 2 * hl, :], R_QK[2 * hl], True))
                specs.append((512 + h * 128, QK[:, 2 * hl + 1, :], R_QK[2 * hl + 1], True))
            proj_group(specs, pj1, pj2)
            run_pipeline(pj1, pj2, 1)
            wk = load_win(1024 + 256 * g, 256)
            for bp in range(8):
                k = 4 + bp % 2
                for bb in range(2):
                    blk = 2 * bp + bb
                    for dc in range(8):
                        mm(pb[k][:, bb * 256:(bb + 1) * 256], hT[:, dc, blk * 128:(blk + 1) * 128], win[:, wk, dc, 0:256],
                           dc == 0, dc == 7, [R_win[wk], R_hT[blk]], [R_pb[k]])
                SC.op("act", lambda e, o=Vd[:, 2 * bp:2 * bp + 2, :], i=pb[k][:].rearrange("p (a b) -> p a b", a=2):
                      e.copy(out=o, in_=i), [R_pb[k]], [R_VV])
            st1, st2 = [], []
            for hl in range(2):
                h = 2 * g + hl
                Qt, Kt = QK[:, 2 * hl, :], QK[:, 2 * hl + 1, :]
                for tq in range(4):
                    nkb = 4 * tq + 4
                    for m in range(nkb):
                        kp = st_i % 2
                        p2 = (st_i % 4) * 2
                        st_i += 1

                        def s1(hl=hl, tq=tq, m=m, kp=kp, p2=p2, Qt=Qt, Kt=Kt):
                            j = m - 4 * tq
                            c0 = 128 * j if j > 0 else 0
                            n = 512 - c0
                            for c in range(2):
                                ps = slice(64 * c, 64 * c + 64)
                                mm(pb[2 * kp + c][:, 0:n], Kt[ps, m * 128:(m + 1) * 128], Qt[ps, tq * 512 + c0:(tq + 1) * 512],
                                   True, True, [R_QK[2 * hl + 1][m // 4], R_QK[2 * hl][tq]], [R_pb[2 * kp + c]])
                            act(PT[:, p2:p2 + 2, 0:n], pall[:, 2 * kp:2 * kp + 2, 0:n], AF.Exp,
                                [R_pb[2 * kp], R_pb[2 * kp + 1]], [R_PT[p2], R_PT[p2 + 1]], scale=0.125)
                            if j >= 0:
                                for c in range(2):
                                    tt("dve", PT[:, p2 + c, 0:128], PT[:, p2 + c, 0:128], maskb[:, 0:128],
                                       ALU.mult, [R_PT[p2 + c], R_c], [R_PT[p2 + c]])

                        def s2(hl=hl, h=h, tq=tq, m=m, nkb=nkb, p2=p2):
                            j = m - 4 * tq
                            c0 = 128 * j if j > 0 else 0
                            n = 512 - c0
                            for c in range(2):
                                mm(pb[4 + c][:, c0:512], Vd[:, m, hl * 128:(hl + 1) * 128], PT[:, p2 + c, 0:n],
                                   m == 0, m == nkb - 1, [R_VV, R_PT[p2 + c]], [R_pb[4 + c]])
                                mm(pb[6 + c][:, c0:512], onesb[:], PT[:, p2 + c, 0:n],
                                   m == 0, m == nkb - 1, [R_c, R_PT[p2 + c]], [R_pb[6 + c]])
                            if m == nkb - 1:
                                for cc in range(2):
                                    tcopy("dve", tmpf[:, 2 + cc, :], pb[4 + cc][:], [R_pb[4 + cc]], [R_tmp[2 + cc]])
                                act(tmpf[:, 0:2, :], pall[:, 6:8, :], AF.Ln, [R_pb[6], R_pb[7]], [R_tmp[0], R_tmp[1]])
                                act(tmpf[:, 0:2, :], tmpf[:, 0:2, :], AF.Exp, [R_tmp[0], R_tmp[1]], [R_tmp[0], R_tmp[1]], scale=-1.0)
                                for cc in range(2):
                                    tt("dve", tmpf[:, 2 + cc, :], tmpf[:, 2 + cc, :], tmpf[:, cc, :], ALU.mult,
                                       [R_tmp[2 + cc], R_tmp[cc]], [R_tmp[2 + cc]])
                                stt("dve", mixT[:, h, tq * 512:(tq + 1) * 512], tmpf[:, 3, :], neglam[:, 0:1], tmpf[:, 2, :],
                                    ALU.mult, ALU.add, [R_tmp[2], R_tmp[3], R_lam], [R_mix[h][tq]])

                        st1.append(s1)
                        st2.append(s2)
            run_pipeline(st1, st2, 3)
        sl1, sl2 = [], []
        idx = 0
        for tq in range(4):
            for h in range(4):
                k = idx % 2
                idx += 1
                msl = mixT[:, h, tq * 512:(tq + 1) * 512]

                def s1(h=h, tq=tq, k=k, msl=msl):
                    act(PT[:, k, :], msl, AF.Square, [R_mix[h][tq]], [R_PT[k]])
                    mm(pb[k][:], onesb[:], PT[:, k, :], True, True, [R_c, R_PT[k]], [R_pb[k]])

                def s2(h=h, tq=tq, k=k, msl=msl):
                    act(tmpf[:, k, :], pb[k][:], AF.Ln, [R_pb[k], R_eps], [R_tmp[k]], bias=epsc[:], scale=1.0 / 128)
                    act(tmpf[:, k, :], tmpf[:, k, :], AF.Exp, [R_tmp[k]], [R_tmp[k]], scale=-0.5)
                    stt("dve", msl, msl, gcol[:, 0:1], tmpf[:, k, :], ALU.mult, ALU.mult,
                        [R_mix[h][tq], R_tmp[k], R_lam], [R_mix[h][tq]])
                    if h == 3:
                        out_proj(0, range(4 * tq, 4 * tq + 4))

                sl1.append(s1)
                sl2.append(s2)
        run_pipeline(sl1, sl2, 1)

    def dil_units(d, qt):
        u = []
        if d == 1:
            for m in range(max(0, 4 * qt - 1), 4 * qt + 4):
                j = m - 4 * qt
                if j == -1:
                    u.append((128 * m, 1, 512 * qt, 1, 0, 128, 128))
                elif j < 3:
                    u.append((128 * m, 1, 512 * qt + 128 * j, 1, 128 * j, 256, 0))
                else:
                    u.append((128 * m, 1, 512 * qt + 384, 1, 384, 128, 0))
        elif d == 4:
            r = qt
            for m in range(4):
                n = 256 if m < 3 else 128
                u.append((512 * m + r, 4, 4 * 128 * m + r, 4, 128 * m, n, 0))
        else:
            for rr in range(4):
                r = 4 * qt + rr
                u.append((r, 16, r, 16, 128 * rr, 128, 0))
        return u

    def dil_stage():
        wout_dma(512)
        SC.op("dve", lambda e: e.memset(vring[:], 1.0), [], R_vr)
        st_i = 0
        vr_i = 0
        nd_i = 0
        pt_i = 0
        for g in range(2):
            pj1, pj2, specs = [], [], []
            for pl in range(2):
                p = 2 * g + pl
                specs.append((1536 + 128 * p, QK[:, 2 * pl, :], R_QK[2 * pl], True))
                specs.append((2048 + 128 * p, QK[:, 2 * pl + 1, :], R_QK[2 * pl + 1], True))
            for pl in range(2):
                p = 2 * g + pl
                specs.append((2560 + 128 * p, gvT[:, pl, :], [R_VV] * 4, False))
            proj_group(specs, pj1, pj2)
            run_pipeline(pj1, pj2, 1)
            for pl in range(2):
                p = 2 * g + pl
                Qt, Kt = QK[:, 2 * pl, :], QK[:, 2 * pl + 1, :]
                st1, st2 = [], []
                for bi, d in enumerate((1, 4, 16)):
                    for qt in range(4):
                        ndk = nd_i % 2
                        nd_i += 1
                        units = dil_units(d, qt)
                        groups, cur, used = [], [], 0
                        for ui, unit in enumerate(units):
                            vs = vr_i % 10
                            vr_i += 1
                            n = unit[5]
                            if used + n > 512:
                                groups.append(cur)
                                cur, used = [], 0
                            cur.append((ui, unit, vs, used))
                            used += n
                        groups.append(cur)
                        tile_units = [(unit, vs) for grp in groups for (ui, unit, vs, off) in grp]
                        kp0 = st_i % 2
                        st_i += 1

                        def pro(pl=pl, tile_units=tile_units, kp=kp0):
                            tb = pb[2 * kp].bitcast(BF16)
                            for i_, (unit, vs) in enumerate(tile_units):
                                k0, kst = unit[0], unit[1]
                                ksl = slice(k0, k0 + 127 * kst + 1, kst)
                                SC.op("pe", lambda e, o=tb[:, i_ * 128:(i_ + 1) * 128], i=gvT[:, pl, ksl]: e.transpose(o, i, ident[:]),
                                      [R_VV, R_ident], [R_pb[2 * kp]])
                            i0 = 0
                            while i0 < len(tile_units):
                                v0 = tile_units[i0][1]
                                cnt = 1
                                while i0 + cnt < len(tile_units) and tile_units[i0 + cnt][1] == v0 + cnt:
                                    cnt += 1
                                SC.op("act", lambda e, o=vring[:, v0:v0 + cnt, :].rearrange("p s (a b) -> p s a b", a=3)[:, :, 0:3:2, :],
                                      i=tb[:, i0 * 128:(i0 + cnt) * 128].rearrange("p (s a b) -> p s a b", s=cnt, a=2):
                                      e.copy(out=o, in_=i), [R_pb[2 * kp]], [R_vr[v] for v in range(v0, v0 + cnt)])
                                i0 += cnt

                        st1.append(pro)
                        st2.append(lambda: None)
                        for gi, grp in enumerate(groups):
                            kp = st_i % 2
                            st_i += 1
                            pp = pt_i % 3
                            pt_i += 1
                            tot = sum(u_[1][5] for u_ in grp)
                            last_grp = gi == len(groups) - 1
                            if pp < 2:
                                PTp = PT[:, 2 * pp:2 * pp + 2, :]
                                R_PTp = [R_PT[2 * pp], R_PT[2 * pp + 1]]
                            else:
                                PTp = PTx.rearrange("p (s t) -> p s t", s=2)
                                R_PTp = [R_PTx]

                            def s1(grp=grp, kp=kp, tot=tot, PTp=PTp, R_PTp=R_PTp, Qt=Qt, Kt=Kt, pl=pl):
                                for (ui, unit, vs, off) in grp:
                                    k0, kst, q0, qst, pc0, n, mc0 = unit
                                    ksl = slice(k0, k0 + 127 * kst + 1, kst)
                                    qsl = slice(q0, q0 + (n - 1) * qst + 1, qst)
                                    for hl in range(2):
                                        ps = slice(64 * hl, 64 * hl + 64)
                                        mm(pb[2 * kp + hl][:, off:off + n], Kt[ps, ksl], Qt[ps, qsl], True, True,
                                           R_QK[2 * pl + 1] + R_QK[2 * pl], [R_pb[2 * kp + hl]])
                                act(PTp[:, :, 0:tot], pall[:, 2 * kp:2 * kp + 2, 0:tot], AF.Exp,
                                    [R_pb[2 * kp], R_pb[2 * kp + 1]], R_PTp, scale=0.125)
                                for si, (ui, unit, vs, off) in enumerate(grp):
                                    k0, kst, q0, qst, pc0, n, mc0 = unit
                                    view = PTp[:, :, off:off + n]
                                    tt("dve", view, view,
                                       maskb[:, mc0:mc0 + n].unsqueeze(1).to_broadcast([128, 2, n]), ALU.mult, R_PTp + [R_c], R_PTp)

                            def s2(p=p, d=d, qt=qt, bi=bi, ndk=ndk, grp=grp, PTp=PTp, R_PTp=R_PTp, last_grp=last_grp):
                                hb_ = (pb[4 + 2 * ndk], pb[5 + 2 * ndk])
                                R_hb_ = (R_pb[4 + 2 * ndk], R_pb[5 + 2 * ndk])
                                for (ui, unit, vs, off) in grp:
                                    k0, kst, q0, qst, pc0, n, mc0 = unit
                                    for hl in range(2):
                                        SC.op("pe", lambda e, o=hb_[hl][:, pc0:pc0 + n], l=vring[:, vs, hl * 64:hl * 64 + 128],
                                              r=PTp[:, hl, off:off + n], s_=(ui == 0):
                                              e.matmul(o, lhsT=l, rhs=r, start=s_, stop=False, skip_group_check=True),
                                              [R_vr[vs]] + R_PTp, [R_hb_[hl]])
                                if not last_grp:
                                    return
                                for hl in range(2):
                                    if d == 1:
                                        dst = acc[:, hl, 512 * qt:512 * (qt + 1)]
                                        src = hb_[hl][:]
                                    elif d == 4:
                                        dst = acc[:, hl, qt:S:4]
                                        src = hb_[hl][:]
                                    else:
                                        dst = acc[:, hl, :].rearrange("p (i r) -> p r i", r=16)[:, 4 * qt:4 * qt + 4, :]
                                        src = hb_[hl][:].rearrange("p (r i) -> p r i", r=4)
                                    if bi == 0:
                                        SC.op("act", lambda e, o=dst, i=src: e.copy(out=o, in_=i), [R_hb_[hl]], [R_acc[hl]])
                                    else:
                                        tt("dve", dst, src, dst, ALU.add, [R_hb_[hl], R_acc[hl]], [R_acc[hl]])
                                if bi == 2 and qt == 3:
                                    for hl in range(2):
                                        nr = slice(64 * hl, 64 * hl + 64)
                                        dr = slice(64 - 64 * hl, 128 - 64 * hl)
                                        rd = tmpf[nr, 0:4, :].rearrange("p a b -> p (a b)")
                                        act(acc[dr, hl, :], acc[dr, hl, :], AF.Ln, [R_acc[hl]], [R_acc[hl]])
                                        act(acc[dr, hl, :], acc[dr, hl, :], AF.Exp, [R_acc[hl]], [R_acc[hl]], scale=-1.0)
                                        tcopy("dve", rd, acc[dr, hl, :], [R_acc[hl]], R_tmp)
                                        for tq in range(4):
                                            sl = slice(tq * 512, (tq + 1) * 512)
                                            tt("dve", mixT[nr, p, sl], acc[nr, hl, sl], rd[:, sl], ALU.mult,
                                               [R_acc[hl]] + R_tmp, [R_mix[p][tq]])

                            st1.append(s1)
                            st2.append(s2)
                run_pipeline(st1, st2, 2)

    def attention(s_, which):
        SC.barrier()
        if not rope_done[0]:
            for op_ in rope_table_ops(s_):
                op_()
        rope_done[0] = False
        if "A" in which:
            diff_stage()
        SC.barrier()
        if "B" in which:
            dil_stage()
            out_proj(512)
        if "ffn2" in stages:
            norm_to_hT(g_ffn2)
        SC.barrier()

    def final_norm(s):
        load_gain(g_fin)
        row_stats(0)
        for t in range(NT):
            k = t % 4
            if t % 4 == 0 and t // 4 + 1 < 4:
                row_stats(t // 4 + 1)
            stt("dve", ostage[:, k, :], x_sb[:, t, :], rstd[:, t:t + 1], gain[:], ALU.mult, ALU.mult,
                [R_x[t], R_rstd[t // 4], R_gain], [R_ostage[k]])
            out_dmas.append(dma("sp", y_out[s, t * 128:(t + 1) * 128, :], ostage[:, k, :], [R_ostage[k]], [],
                                ("out", k)))

    for s in range(nseq):
        if s > 0:
            load_x(s)
        if s == 0 and "ffn1" not in stages:
            load_constants()
            consts_done[0] = True
        if "ffn1" in stages:
            norm_to_hT(g_ffn1)
            if s == 0:
                load_constants()
                consts_done[0] = True
            with_attn = any(st in stages for st in ("attn", "attnA", "attnB"))
            ffn(w1g, w1u, w1d, extra=rope_table_ops(s) if with_attn else None)
            rope_done[0] = with_attn
        if any(st in stages for st in ("attn", "attnA", "attnB")):
            norm_to_hT(g_mix)
        if "attn" in stages:
            attention(s, "AB")
        elif "attnA" in stages:
            attention(s, "A")
        elif "attnB" in stages:
            attention(s, "B")
        if "ffn2" in stages:
            if not any(st in stages for st in ("attn", "attnA", "attnB")):
                norm_to_hT(g_ffn2)
            ffn(w2g, w2u, w2d)
        final_norm(s)

    sem_eng = {e: es.enter_context(nc.semaphore(f"s_{e}")) for e in Sched.ENGS}
    sem_dma = {}
    for i, k in enumerate(SC.dma_keys):
        sem_dma[k] = es.enter_context(nc.semaphore(f"d_{i}"))
    SC.emit(nc, None, sem_eng, sem_dma)
    finals = {}
    for o in out_dmas:
        sem, val = o.signal
        finals[id(sem)] = (sem, max(val, finals.get(id(sem), (sem, 0))[1]))
    block = es.enter_context(nc.Block())

    @block.sync
    def _(e):
        SC.run_engine("sp", e, final_waits=list(finals.values()))

    @block.gpsimd
    def _(e):
        SC.run_engine("pool", e)

    @block.scalar
    def _(e):
        SC.run_engine("act", e)

    @block.vector
    def _(e):
        SC.run_engine("dve", e)

    @block.tensor
    def _(e):
        SC.run_engine("pe", e)

    es.close()
    return nc


def make_consts():
    perm = np.zeros((128, 128), np.float32)
    invf = np.zeros((128, 1), np.float32)
    inv = np.exp(-math.log(500000.0) * np.arange(8, dtype=np.float32) * 2.0 / 16).astype(np.float32)
    for base in (0, 64):
        for j in range(8):
            perm[base + j + 8, base + j] = -1.0
            perm[base + j, base + j + 8] = 1.0
            invf[base + j, 0] = inv[j]
            invf[base + j + 8, 0] = inv[j]
    kk = np.arange(128)[:, None]
    qq = np.arange(128)[None, :]
    mask = np.concatenate([(qq >= kk), (qq <= kk)], axis=1).astype(np.float32)
    return {
        "c_ident": np.eye(128, dtype=np.float32).astype(ml_dtypes.bfloat16),
        "c_perm": perm.astype(ml_dtypes.bfloat16),
        "c_mask": mask.astype(ml_dtypes.bfloat16),
        "c_invf": invf,
    }


def make_in_maps(inputs, nseq=2, ncores=NCORES):
    consts = make_consts()
    w = {}
    for k, v in inputs.items():
        if k in ("x", "positions"):
            continue
        a = np.asarray(v)
        if k == "final_norm":
            a = a.reshape(1, D)
        else:
            a = a.reshape(a.shape[1:]) if a.ndim == 3 else a
        w[k] = np.ascontiguousarray(a)
    x = np.asarray(inputs["x"])
    pos = np.asarray(inputs["positions"]).astype(np.int32)
    maps = []
    for c in range(ncores):
        m = dict(w)
        m.update(consts)
        m["x"] = np.ascontiguousarray(x[c * nseq:(c + 1) * nseq])
        m["positions"] = np.ascontiguousarray(pos[c * nseq:(c + 1) * nseq])
        maps.append(m)
    return maps


def kernel(**inputs):
    nc = build_nc(nseq=2)
    maps = make_in_maps(inputs, nseq=2, ncores=NCORES)
    res = run_bass_kernel_spmd(nc, maps, core_ids=list(range(NCORES)))
    return np.concatenate([np.asarray(r["y"]) for r in res.results], axis=0).astype(np.float32)
```

```python
import math
import numpy as np
import ml_dtypes
import concourse.bass as bass
import concourse.mybir as mybir
from concourse.bass_utils import run_bass_kernel_spmd

F32 = mybir.dt.float32
BF16 = mybir.dt.bfloat16
I32 = mybir.dt.int32
ALU = mybir.AluOpType
AF = mybir.ActivationFunctionType

NCORES = 8
D = 1024
S = 2048
DFF = 2816
NCH = DFF // 128
NT = S // 128
EPS = 1e-5
G = 4
GROUP_END = {2: 0, 5: 3, 9: 6, 13: 10, 17: 14, 21: 18}
LAMBDA_INIT = 0.8 - 0.6 * math.exp(-0.3 * 0)


class Res:
    __slots__ = ("name", "last_w", "readers", "excl", "gen", "dgen")

    def __init__(self, name, excl=False):
        self.name = name
        self.last_w = None
        self.readers = []
        self.excl = excl
        self.gen = 0
        self.dgen = 0


class Op:
    __slots__ = ("eng", "fn", "deps", "dma_key", "need_inc", "signal", "idx")

    def __init__(self, eng, fn, dma_key):
        self.eng = eng
        self.fn = fn
        self.dma_key = dma_key
        self.deps = []
        self.need_inc = dma_key is not None
        self.signal = None


class Sched:
    ENGS = ("pe", "act", "dve", "pool", "sp")

    def __init__(self):
        self.ops = {e: [] for e in self.ENGS}
        self.dma_keys = {}
        self.final_deps = []
        self.pending = {}
        self.n_ops = 0
        self.verify = None
        self.base_gen = {}
        self.cur_rec = None
        self.dry = False
        self.expect = None
        self.expect_pos = 0

    def barrier(self):
        lasts = [self.ops[e][-1] for e in self.ENGS if self.ops[e]]
        dl = {}
        for e in self.ENGS:
            for o in self.ops[e]:
                if o.dma_key is not None and (o.dma_key not in dl or dl[o.dma_key].idx < o.idx):
                    dl[o.dma_key] = o
        allp = lasts + list(dl.values())
        self.pending = {e: list(allp) for e in self.ENGS}

    def op(self, eng, fn, reads=(), writes=(), dma_key=None):
        if self.dry:
            rec = []
            for r in list(reads) + list(writes):
                if r.dgen == 0 and r not in self._touched:
                    self._touched.append(r)
            for r in reads:
                rec.append((r, r.dgen))
            for w in writes:
                w.dgen += 1
            self.cur_rec.append(rec)
            return None
        if self.verify is not None:
            rec = self.verify.pop(0)
            for (r, g) in rec:
                if r.gen - self.base_gen[id(r)] != g:
                    raise AssertionError(
                        f"PIPELINE HAZARD on {r.name}: read expects write #{g}, emission order gives #{r.gen - self.base_gen[id(r)]}")
        for w in writes:
            w.gen += 1
        o = Op(eng, fn, dma_key)
        deps = {}

        def add(d, raw):
            if d is None:
                return
            if d.dma_key is None and o.dma_key is None and d.eng == eng:
                if eng == "pe":
                    return
            deps[id(d)] = d

        for r in reads:
            if r.excl:
                continue
            add(r.last_w, True)
        wl = list(writes) + [r for r in reads if r.excl]
        for w in wl:
            add(w.last_w, True)
            for rd in w.readers:
                add(rd, False)
        if self.pending.get(eng):
            for d in self.pending[eng]:
                if d.dma_key is None and d.eng == eng:
                    continue
                deps[id(d)] = d
            self.pending[eng] = []
        o.deps = list(deps.values())
        for d in o.deps:
            d.need_inc = True
        for r in reads:
            if not r.excl:
                r.readers.append(o)
        for w in wl:
            w.last_w = o
            w.readers = []
        if dma_key is not None and dma_key not in self.dma_keys:
            self.dma_keys[dma_key] = None
        o.idx = self.n_ops
        self.n_ops += 1
        self.ops[eng].append(o)
        return o

    def emit(self, nc, block_engines, sems_eng, sems_dma):
        for e in self.ENGS:
            cnt = 0
            for o in self.ops[e]:
                if o.dma_key is None and o.need_inc:
                    cnt += 1
                    o.signal = (sems_eng[e], cnt)
        dcnt = {k: 0 for k in sems_dma}
        dmas = [o for e in self.ENGS for o in self.ops[e] if o.dma_key is not None]
        for o in sorted(dmas, key=lambda o: o.idx):
            dcnt[o.dma_key] += 16
            o.signal = (sems_dma[o.dma_key], dcnt[o.dma_key])
        return dcnt

    def run_engine(self, e, eng, final_waits=None):
        waited = {}
        for o in self.ops[e]:
            for d in o.deps:
                sem, val = d.signal
                k = id(sem)
                if waited.get(k, 0) < val:
                    eng.wait_ge(sem, val)
                    waited[k] = val
            ins = o.fn(eng)
            if o.need_inc:
                sem, _ = o.signal
                ins.then_inc(sem, 16 if o.dma_key is not None else 1)
        if final_waits:
            for sem, val in final_waits:
                if waited.get(id(sem), 0) < val:
                    eng.wait_ge(sem, val)


def build_nc(nseq=2, stages=("ffn1", "attn", "ffn2"), debug_out=None):
    nc = bass.Bass("TRN2", target_bir_lowering=False)
    SC = Sched()

    def din(name, shape, dt=F32):
        return nc.dram_tensor(name, list(shape), dt, kind="ExternalInput").ap()

    x_in = din("x", [nseq, S, D])
    pos_in = din("positions", [nseq, S], I32)
    g_ffn1 = din("ffn1_norm", [1, D])
    w1g = din("ffn1_gate", [D, DFF])
    w1u = din("ffn1_up", [D, DFF])
    w1d = din("ffn1_down", [DFF, D])
    g_mix = din("mix_norm", [1, D])
    w_in = din("w_in", [D, 3072])
    lq1 = din("lambda_q1", [1, 64])
    lk1 = din("lambda_k1", [1, 64])
    lq2 = din("lambda_q2", [1, 64])
    lk2 = din("lambda_k2", [1, 64])
    subln = din("subln_gain", [1, 128])
    w_out = din("w_out", [D, D])
    g_ffn2 = din("ffn2_norm", [1, D])
    w2g = din("ffn2_gate", [D, DFF])
    w2u = din("ffn2_up", [D, DFF])
    w2d = din("ffn2_down", [DFF, D])
    g_fin = din("final_norm", [1, D])
    c_ident = din("c_ident", [128, 128], BF16)
    c_perm = din("c_perm", [128, 128], BF16)
    c_mask = din("c_mask", [128, 256], BF16)
    c_invf = din("c_invf", [128, 1], F32)
    y_out = nc.dram_tensor("y", [nseq, S, D], F32, kind="ExternalOutput").ap()

    from contextlib import ExitStack
    es = ExitStack()

    def sb(name, shape, dt):
        return es.enter_context(nc.sbuf_tensor(name, list(shape), dt))

    def psum(name, shape, dt):
        return es.enter_context(nc.psum_tensor(name, list(shape), dt))

    x_sb = sb("x_sb", [128, NT, D], F32)
    hT = sb("hT", [128, 8, S], BF16)
    hb = sb("hb", [128, 2, D], BF16)
    junk = sb("junk", [128, D], BF16)
    ss = sb("ss", [128, NT], F32)
    rstd = sb("rstd", [128, NT], F32)
    ident = sb("ident", [128, 128], BF16)
    perm = sb("perm", [128, 128], BF16)
    maskb = sb("maskb", [128, 256], BF16)
    onesb = sb("onesb", [128, 128], BF16)
    invf = sb("invf", [128, 1], F32)
    epsc = sb("epsc", [128, 1], F32)
    lamt = sb("lamt", [128, 4, 64], F32)
    lams = sb("lams", [128, 4], F32)
    neglam = sb("neglam", [128, 1], F32)
    gcol = sb("gcol", [128, 1], F32)
    acc = sb("acc", [128, 2, S], F32)
    tmpf = sb("tmpf", [128, 6, 512], F32)
    U16 = sb("U16", [128, 37888], BF16)
    gain = tmpf[:, 4:6, :].rearrange("p a b -> p (a b)")
    sg = tmpf[:, 0:2, :]
    ostage = acc[:].rearrange("p s (a b) -> p (s a) b", a=2)
    posi = acc[:, 1, 0:1024].bitcast(I32).rearrange("p (a b) -> p a b", a=2)
    qraw = junk[:].rearrange("p (a b) -> p a b", a=2)
    wg_r = U16[:, 0:3072].rearrange("p (s c f) -> p s c f", s=3, c=8)
    wu_r = U16[:, 3072:6144].rearrange("p (s c f) -> p s c f", s=3, c=8)
    wd_r = U16[:, 6144:12288].rearrange("p (s f) -> p s f", s=6)
    aT = U16[:, 12288:28672].rearrange("p (s t) -> p s t", s=8)
    mixT = U16[:, 0:8192].rearrange("p (s t) -> p s t", s=4)
    QK = U16[:, 8192:16384].rearrange("p (s t) -> p s t", s=4)
    VV = U16[:, 16384:20480]
    Vd = VV.rearrange("p (b c) -> p b c", b=16)
    gvT = VV.rearrange("p (s t) -> p s t", s=2)
    PT = U16[:, 20480:24576].rearrange("p (s t) -> p s t", s=8)
    win = U16[:, 24576:28672].rearrange("p (s c f) -> p s c f", s=2, c=8)
    wout = U16[:, 28672:32768].rearrange("p (s f) -> p s f", s=4)
    CS = U16[:, 32768:36864].rearrange("p (s t) -> p s t", s=2)
    PTx = U16[:, 36864:37888]
    vring = U16[:, 22528:24448].rearrange("p (s f) -> p s f", s=10)

    pall = psum("pall", [128, 8, 512], F32)
    pb = [pall[:, i, :] for i in range(8)]
    pt = [pb[6].bitcast(BF16), pb[7].bitcast(BF16)]

    R_x = [Res(f"x{t}") for t in range(NT)]
    R_hT = [Res(f"hT{t}") for t in range(NT)]
    R_gain = Res("gain")
    R_hb = [Res("hb0"), Res("hb1")]
    R_junk = Res("junk")
    R_ss = [Res(f"ss{q}") for q in range(4)]
    R_rstd = [Res(f"rstd{q}") for q in range(4)]
    R_eps = Res("eps")
    R_ident = Res("ident")
    R_ostage = [Res(f"os{i}") for i in range(4)]
    R_sg = [Res("sg0"), Res("sg1")]
    R_wg = [Res(f"wg{i}") for i in range(3)]
    R_wu = [Res(f"wu{i}") for i in range(3)]
    R_wd = [Res(f"wd{i}") for i in range(6)]
    R_aT = [[Res(f"aT{i}_{t}") for t in range(4)] for i in range(8)]
    R_pb = [Res(f"pb{i}", excl=True) for i in range(8)]
    R_pt = [R_pb[6], R_pb[7]]
    R_c = Res("consts")
    R_lam = Res("lam")
    R_posi = [R_ostage[2], R_ostage[2]]
    R_tmp = [Res(f"tmp{i}") for i in range(4)]
    R_CS = [Res(f"CS{i}") for i in range(4)]
    R_qraw = [Res("qraw0"), Res("qraw1")]
    R_win = [Res("win0"), Res("win1")]
    R_QK = [[Res(f"QK{i}_{t}") for t in range(4)] for i in range(4)]
    R_VV = Res("VV")
    R_PT = [Res(f"PT{i}") for i in range(8)]
    R_PTx = Res("PTx")
    R_vr = [Res(f"vr{i}") for i in range(10)]
    R_mix = [[Res(f"mix{i}_{t}") for t in range(4)] for i in range(4)]
    R_wout = [Res(f"wout{i}") for i in range(4)]
    R_acc = [Res("accN"), Res("accD")]
    out_dmas = []

    def dma(eng, out, in_, reads, writes, key):
        return SC.op(eng, lambda e, o=out, i=in_: e.dma_start(out=o, in_=i), reads, writes, dma_key=key)

    def mm(out, lhsT, rhs, start, stop, reads, writes):
        return SC.op("pe", lambda e, o=out, l=lhsT, r=rhs, s=start, t=stop:
                     e.matmul(o, lhsT=l, rhs=r, start=s, stop=t), reads, writes)

    def act(out, in_, func, reads, writes, bias=0.0, scale=1.0, accum_out=None):
        if accum_out is None:
            return SC.op("act", lambda e, o=out, i=in_, f=func, b=bias, s=scale:
                         e.activation(out=o, in_=i, func=f, bias=b, scale=s), reads, writes)
        return SC.op("act", lambda e, o=out, i=in_, f=func, b=bias, s=scale, a=accum_out:
                     e.activation(out=o, in_=i, func=f, bias=b, scale=s, accum_out=a), reads, writes)

    def tscalar(eng, out, in0, s1, s2, op0, op1, reads, writes):
        if op1 is None:
            return SC.op(eng, lambda e, o=out, i=in0, a=s1, p=op0:
                         e.tensor_scalar(out=o, in0=i, scalar1=a, scalar2=None, op0=p), reads, writes)
        return SC.op(eng, lambda e, o=out, i=in0, a=s1, b=s2, p=op0, q=op1:
                     e.tensor_scalar(out=o, in0=i, scalar1=a, scalar2=b, op0=p, op1=q), reads, writes)

    def stt(eng, out, in0, scalar, in1, op0, op1, reads, writes):
        return SC.op(eng, lambda e, o=out, i=in0, s=scalar, j=in1, p=op0, q=op1:
                     e.scalar_tensor_tensor(out=o, in0=i, scalar=s, in1=j, op0=p, op1=q), reads, writes)

    def tt(eng, out, in0, in1, op, reads, writes):
        return SC.op(eng, lambda e, o=out, i=in0, j=in1, p=op:
                     e.tensor_tensor(out=o, in0=i, in1=j, op=p), reads, writes)

    def tcopy(eng, out, in_, reads, writes):
        return SC.op(eng, lambda e, o=out, i=in_: e.tensor_copy(out=o, in_=i), reads, writes)

    dma("sp", ident[:], c_ident, [], [R_ident], "c_ident")
    SC.op("dve", lambda e: e.memset(epsc[:], EPS), [], [R_eps])

    def load_x(s_):
        if s_ == 0:
            for q4 in range(4):
                dma("sp", x_sb[:, 4 * q4:4 * q4 + 4, :],
                    x_in[s_, 512 * q4:512 * (q4 + 1), :].rearrange("(t p) d -> p t d", p=128),
                    [], R_x[4 * q4:4 * q4 + 4], ("xq", q4))
            return
        for t in range(NT):
            dma("act", x_sb[:, t, :], x_in[s_, t * 128:(t + 1) * 128, :], [], [R_x[t]], ("x", t))

    load_x(0)

    def load_constants():
        dma("sp", perm[:], c_perm, [], [R_c], "c_perm")
        dma("sp", maskb[:], c_mask, [], [R_c], "c_mask")
        dma("sp", invf[:], c_invf, [], [R_c], "c_invf")
        SC.op("dve", lambda e: e.memset(onesb[:], 1.0), [], [R_c])
        for i, lap in enumerate((lq1, lk1, lq2, lk2)):
            dma("sp", lamt[:, i, :], lap.partition_broadcast(128), [], [R_lam], ("lam", i))
        tt("dve", lamt[:, 0, :], lamt[:, 0, :], lamt[:, 1, :], ALU.mult, [R_lam], [R_lam])
        tt("dve", lamt[:, 2, :], lamt[:, 2, :], lamt[:, 3, :], ALU.mult, [R_lam], [R_lam])
        SC.op("dve", lambda e: e.reduce_sum(out=lams[:, 0:1], in_=lamt[:, 0, :], axis=mybir.AxisListType.X), [R_lam], [R_lam])
        SC.op("dve", lambda e: e.reduce_sum(out=lams[:, 1:2], in_=lamt[:, 2, :], axis=mybir.AxisListType.X), [R_lam], [R_lam])
        act(lams[:, 2:4], lams[:, 0:2], AF.Exp, [R_lam], [R_lam])
        tt("dve", lams[:, 0:1], lams[:, 3:4], lams[:, 2:3], ALU.subtract, [R_lam], [R_lam])
        tscalar("dve", neglam[:], lams[:, 0:1], -LAMBDA_INIT, None, ALU.add, None, [R_lam], [R_lam])
        dma("sp", gcol[:], subln.rearrange("o e -> e o"), [], [R_lam], "gcol")
        tscalar("dve", gcol[:], gcol[:], 1.0 - LAMBDA_INIT, None, ALU.mult, None, [R_lam], [R_lam])

    def load_gain(g_ap):
        dma("sp", gain[:], g_ap.partition_broadcast(128), [], [R_gain], "gain")

    def row_stats(q):
        sl = slice(4 * q, 4 * q + 4)
        for t in range(4 * q, 4 * q + 4):
            act(junk[:], x_sb[:, t, :], AF.Square, [R_x[t]], [R_junk, R_ss[q]], accum_out=ss[:, t:t + 1])
        act(rstd[:, sl], ss[:, sl], AF.Sqrt, [R_ss[q], R_eps], [R_rstd[q]], bias=epsc[:], scale=1.0 / D)
        SC.op("dve", lambda e: e.reciprocal(out=rstd[:, sl], in_=rstd[:, sl]), [R_rstd[q]], [R_rstd[q]])

    def norm_to_hT(g_ap):
        load_gain(g_ap)
        row_stats(0)
        for t in range(NT):
            k = t % 2
            if t % 4 == 0 and t // 4 + 1 < 4:
                row_stats(t // 4 + 1)
            stt("dve", hb[:, k, :], x_sb[:, t, :], rstd[:, t:t + 1], gain[:], ALU.mult, ALU.mult,
                [R_x[t], R_rstd[t // 4], R_gain], [R_hb[k]])
            for dc in range(8):
                SC.op("pe", lambda e, o=pt[k][:, dc * 128:(dc + 1) * 128], i=hb[:, k, dc * 128:(dc + 1) * 128]:
                      e.transpose(o, i, ident[:]), [R_hb[k], R_ident], [R_pt[k]])
            src = pt[k].rearrange("p (c t) -> p c t", c=8)
            if t % 2 == 0:
                SC.op("act", lambda e, o=hT[:, :, t * 128:(t + 1) * 128], i=src: e.copy(out=o, in_=i),
                      [R_pt[k]], [R_hT[t]])
            else:
                tcopy("dve", hT[:, :, t * 128:(t + 1) * 128], src, [R_pt[k]], [R_hT[t]])

    def ffn(wg_d, wu_d, wd_d, extra=None):
        extra = list(extra or [])
        wg_v = wg_d.rearrange("(dc p) f -> p dc f", p=128)
        wu_v = wu_d.rearrange("(dc p) f -> p dc f", p=128)
        it = 0
        yi = 0
        for c in range(NCH):
            s3, s6, s8 = c % 3, c % 6, c % 8
            dma("pool", wg_r[:, s3], wg_v[:, :, c * 128:(c + 1) * 128], [], [R_wg[s3]], ("wg", s3))
            dma("pool", wu_r[:, s3], wu_v[:, :, c * 128:(c + 1) * 128], [], [R_wu[s3]], ("wu", s3))
            dma("pool", wd_r[:, s6], wd_d[c * 128:(c + 1) * 128, :], [], [R_wd[s6]], ("wd", s6))
            for tq in range(4):
                k = it % 2
                it += 1
                gb, ub = pb[k], pb[2 + k]
                hres = R_hT[tq * 4:(tq + 1) * 4]
                for dc in range(8):
                    mm(gb[:], wg_r[:, s3, dc, :], hT[:, dc, tq * 512:(tq + 1) * 512], dc == 0, dc == 7,
                       [R_wg[s3]] + hres, [R_pb[k]])
                for dc in range(8):
                    mm(ub[:], wu_r[:, s3, dc, :], hT[:, dc, tq * 512:(tq + 1) * 512], dc == 0, dc == 7,
                       [R_wu[s3]] + hres, [R_pb[2 + k]])
                if extra and c >= 1:
                    extra.pop(0)()
                act(sg[:, k, :], gb[:], AF.Silu, [R_pb[k]], [R_sg[k]])
                tt("dve", aT[:, s8, tq * 512:(tq + 1) * 512], sg[:, k, :], ub[:], ALU.mult,
                   [R_sg[k], R_pb[2 + k]], [R_aT[s8][tq]])
            if c in GROUP_END:
                grp = list(range(GROUP_END[c], c + 1))
                for b in range(NT):
                    for dh in range(2):
                        k = yi % 2
                        yi += 1
                        yb = pb[4 + k]
                        for j, cc in enumerate(grp):
                            mm(yb[:], aT[:, cc % 8, b * 128:(b + 1) * 128], wd_r[:, cc % 6, dh * 512:(dh + 1) * 512],
                               j == 0, j == len(grp) - 1, [R_aT[cc % 8][b // 4], R_wd[cc % 6]], [R_pb[4 + k]])
                        xs = x_sb[:, b, dh * 512:(dh + 1) * 512]
                        stt("dve", xs, yb[:], 0.5, xs, ALU.mult, ALU.add, [R_pb[4 + k], R_x[b]], [R_x[b]])
        while extra:
            extra.pop(0)()


    PI = math.pi
    rope_done = [False]
    consts_done = [False]
    PIC = 3.1415925
    pe_cnt = [0, 0, 0]

    def run_pipeline(st1, st2, depth):
        n = len(st1)
        recs1, recs2, touched = [], [], []
        SC.dry, SC._touched = True, touched
        for i in range(n):
            SC.cur_rec = []
            st1[i]()
            recs1.append(SC.cur_rec)
            SC.cur_rec = []
            st2[i]()
            recs2.append(SC.cur_rec)
        SC.dry = False
        for r in touched:
            r.dgen = 0
        SC.base_gen = {id(r): r.gen for r in touched}
        for i in range(n + depth):
            if i < n:
                SC.verify = list(recs1[i])
                st1[i]()
            if i >= depth:
                SC.verify = list(recs2[i - depth])
                st2[i - depth]()
        SC.verify = None

    def rope_table_ops(s_):
        ops = []
        tv = acc[:, 0, :].rearrange("p (a b) -> p a b", a=4)
        R_t = [R_ostage[0], R_ostage[0], R_ostage[1], R_ostage[1]]
        for tq in range(4):
            k = tq % 2
            ang, ni, r_, m_ = tv[:, 0, :], tv[:, 1, :].bitcast(I32), tv[:, 2, :], tv[:, 3, :]
            ops.append(lambda tq=tq, k=k: dma("sp", posi[:, k, :],
                                              pos_in[s_:s_ + 1, tq * 512:(tq + 1) * 512].partition_broadcast(128),
                                              [], [R_posi[k]], ("posi", k)))
            ops.append(lambda k=k, ang=ang: tscalar("dve", ang, posi[:, k, :], invf[:, 0:1], None, ALU.mult, None,
                                                    [R_posi[k], R_c], [R_t[0]]))
            ops.append(lambda ang=ang, ni=ni: tscalar("dve", ni, ang, 1.0 / (2 * PI), None, ALU.mult, None, [R_t[0]], [R_t[1]]))
            ops.append(lambda ni=ni, m_=m_: tscalar("dve", m_, ni, -2 * PI, None, ALU.mult, None, [R_t[1]], [R_t[3]]))
            ops.append(lambda ang=ang, r_=r_, m_=m_: tt("dve", r_, m_, ang, ALU.add, [R_t[3], R_t[0]], [R_t[2]]))
            for which in (1, 0):
                if which == 0:
                    ops.append(lambda r_=r_: tscalar("dve", r_, r_, PI / 2, None, ALU.add, None, [R_t[2]], [R_t[2]]))
                ops.append(lambda r_=r_, m_=m_: tscalar("dve", m_, r_, PIC, -2 * PI, ALU.is_gt, ALU.mult, [R_t[2]], [R_t[3]]))
                ops.append(lambda r_=r_, m_=m_: tt("dve", r_, r_, m_, ALU.add, [R_t[2], R_t[3]], [R_t[2]]))
                ops.append(lambda r_=r_: tscalar("dve", r_, r_, -PIC, PIC, ALU.max, ALU.min, [R_t[2]], [R_t[2]]))
                ops.append(lambda r_=r_, which=which, tq=tq: act(CS[:, which, tq * 512:(tq + 1) * 512], r_, AF.Sin,
                                                                  [R_t[2]], [R_CS[tq]]))
        return ops

    def load_win(col0, ncols):
        k = pe_cnt[0] % 2
        pe_cnt[0] += 1
        win_dma(k, col0, ncols)
        return k

    w_in_v = w_in.rearrange("(dc p) f -> p dc f", p=128)

    def win_dma(wk, col0, ncols):
        dma("pool", win[:, wk, :, 0:ncols], w_in_v[:, :, col0:col0 + ncols], [], [R_win[wk]], ("win", wk))

    def proj_group(specs, st1, st2):
        slots = []
        for _ in specs:
            slots.append(pe_cnt[0] % 2)
            pe_cnt[0] += 1
        win_dma(slots[0], specs[0][0], 128)
        for j, (col0, dst, dres, rope) in enumerate(specs):
            wk = slots[j]
            for tq in range(4):
                k = pe_cnt[1] % 2
                pe_cnt[1] += 1

                def s1(j=j, tq=tq, k=k, wk=wk, dst=dst, dres=dres, rope=rope):
                    if tq == 0 and j + 1 < len(specs):
                        win_dma(slots[j + 1], specs[j + 1][0], 128)
                    A = pb[k]
                    hres = R_hT[tq * 4:(tq + 1) * 4]
                    for dc in range(8):
                        mm(A[:], win[:, wk, dc, 0:128], hT[:, dc, tq * 512:(tq + 1) * 512], dc == 0, dc == 7,
                           [R_win[wk]] + hres, [R_pb[k]])
                    if not rope:
                        SC.op("act", lambda e, o=dst[:, tq * 512:(tq + 1) * 512], i=A[:]: e.copy(out=o, in_=i),
                              [R_pb[k]], [dres[tq]])
                    else:
                        SC.op("act", lambda e, o=qraw[:, k, :], i=A[:]: e.copy(out=o, in_=i), [R_pb[k]], [R_qraw[k]])

                def s2(tq=tq, k=k, dst=dst, dres=dres, rope=rope):
                    if not rope:
                        return
                    dsl = dst[:, tq * 512:(tq + 1) * 512]
                    B = pb[2 + k]
                    mm(B[:], perm[:], qraw[:, k, :], True, True, [R_c, R_qraw[k]], [R_pb[2 + k]])
                    t1, t2 = tmpf[:, k, :], tmpf[:, 2 + k, :]
                    tt("dve", t1, qraw[:, k, :], CS[:, 0, tq * 512:(tq + 1) * 512], ALU.mult, [R_qraw[k], R_CS[tq]], [R_tmp[k]])
                    tt("dve", t2, B[:], CS[:, 1, tq * 512:(tq + 1) * 512], ALU.mult, [R_pb[2 + k], R_CS[tq]], [R_tmp[2 + k]])
                    tt("pool", dsl, t1, t2, ALU.add, [R_tmp[k], R_tmp[2 + k]], [dres[tq]])

                st1.append(s1)
                st2.append(s2)

    def wout_dma(row0):
        for j in range(4):
            dma("pool", wout[:, j, :], w_out[row0 + j * 128:row0 + (j + 1) * 128, :], [], [R_wout[j]], ("wout", j))

    def out_proj(row0, blocks=None):
        yi = 0
        for b in (range(NT) if blocks is None else blocks):
            for dh in range(2):
                k = yi % 2
                yi += 1
                yb = pb[4 + k]
                for j in range(4):
                    mm(yb[:], mixT[:, j, b * 128:(b + 1) * 128], wout[:, j, dh * 512:(dh + 1) * 512], j == 0, j == 3,
                       [R_mix[j][b // 4], R_wout[j]], [R_pb[4 + k]])
                xs = x_sb[:, b, dh * 512:(dh + 1) * 512]
                tt("dve", xs, yb[:], xs, ALU.add, [R_pb[4 + k], R_x[b]], [R_x[b]])

    def diff_stage():
        wout_dma(0)
        st_i = 0
        for g in range(2):
            pj1, pj2, specs = [], [], []
            for hl in range(2):
                h = 2 * g + hl
                specs.append((h * 128, QK[:, 2 * hl, :], R_QK[2 * hl], True))
                specs.append((512 + h * 128, QK[:, 2 * hl + 1, :], R_QK[2 * hl + 1], True))
            proj_group(specs, pj1, pj2)
            run_pipeline(pj1, pj2, 1)
            wk = load_win(1024 + 256 * g, 256)
            for bp in range(8):
                k = 4 + bp % 2
                for bb in range(2):
                    blk = 2 * bp + bb
                    for dc in range(8):
                        mm(pb[k][:, bb * 256:(bb + 1) * 256], hT[:, dc, blk * 128:(blk + 1) * 128], win[:, wk, dc, 0:256],
                           dc == 0, dc == 7, [R_win[wk], R_hT[blk]], [R_pb[k]])
                SC.op("act", lambda e, o=Vd[:, 2 * bp:2 * bp + 2, :], i=pb[k][:].rearrange("p (a b) -> p a b", a=2):
                      e.copy(out=o, in_=i), [R_pb[k]], [R_VV])
            st1, st2 = [], []
            for hl in range(2):
                h = 2 * g + hl
                Qt, Kt = QK[:, 2 * hl, :], QK[:, 2 * hl + 1, :]
                for tq in range(4):
                    nkb = 4 * tq + 4
                    for m in range(nkb):
                        kp = st_i % 2
                        p2 = (st_i % 4) * 2
                        st_i += 1

                        def s1(hl=hl, tq=tq, m=m, kp=kp, p2=p2, Qt=Qt, Kt=Kt):
                            j = m - 4 * tq
                            c0 = 128 * j if j > 0 else 0
                            n = 512 - c0
                            for c in range(2):
                                ps = slice(64 * c, 64 * c + 64)
                                mm(pb[2 * kp + c][:, 0:n], Kt[ps, m * 128:(m + 1) * 128], Qt[ps, tq * 512 + c0:(tq + 1) * 512],
                                   True, True, [R_QK[2 * hl + 1][m // 4], R_QK[2 * hl][tq]], [R_pb[2 * kp + c]])
                            act(PT[:, p2:p2 + 2, 0:n], pall[:, 2 * kp:2 * kp + 2, 0:n], AF.Exp,
                                [R_pb[2 * kp], R_pb[2 * kp + 1]], [R_PT[p2], R_PT[p2 + 1]], scale=0.125)
                            if j >= 0:
                                for c in range(2):
                                    tt("dve", PT[:, p2 + c, 0:128], PT[:, p2 + c, 0:128], maskb[:, 0:128],
                                       ALU.mult, [R_PT[p2 + c], R_c], [R_PT[p2 + c]])

                        def s2(hl=hl, h=h, tq=tq, m=m, nkb=nkb, p2=p2):
                            j = m - 4 * tq
                            c0 = 128 * j if j > 0 else 0
                            n = 512 - c0
                            for c in range(2):
                                mm(pb[4 + c][:, c0:512], Vd[:, m, hl * 128:(hl + 1) * 128], PT[:, p2 + c, 0:n],
                                   m == 0, m == nkb - 1, [R_VV, R_PT[p2 + c]], [R_pb[4 + c]])
                                mm(pb[6 + c][:, c0:512], onesb[:], PT[:, p2 + c, 0:n],
                                   m == 0, m == nkb - 1, [R_c, R_PT[p2 + c]], [R_pb[6 + c]])
                            if m == nkb - 1:
                                for cc in range(2):
                                    tcopy("dve", tmpf[:, 2 + cc, :], pb[4 + cc][:], [R_pb[4 + cc]], [R_tmp[2 + cc]])
                                act(tmpf[:, 0:2, :], pall[:, 6:8, :], AF.Ln, [R_pb[6], R_pb[7]], [R_tmp[0], R_tmp[1]])
                                act(tmpf[:, 0:2, :], tmpf[:, 0:2, :], AF.Exp, [R_tmp[0], R_tmp[1]], [R_tmp[0], R_tmp[1]], scale=-1.0)
                                for cc in range(2):
                                    tt("dve", tmpf[:, 2 + cc, :], tmpf[:, 2 + cc, :], tmpf[:, cc, :], ALU.mult,
                                       [R_tmp[2 + cc], R_tmp[cc]], [R_tmp[2 + cc]])
                                stt("dve", mixT[:, h, tq * 512:(tq + 1) * 512], tmpf[:, 3, :], neglam[:, 0:1], tmpf[:, 2, :],
                                    ALU.mult, ALU.add, [R_tmp[2], R_tmp[3], R_lam], [R_mix[h][tq]])

                        st1.append(s1)
                        st2.append(s2)
            run_pipeline(st1, st2, 3)
        sl1, sl2 = [], []
        idx = 0
        for tq in range(4):
            for h in range(4):
                k = idx % 2
                idx += 1
                msl = mixT[:, h, tq * 512:(tq + 1) * 512]

                def s1(h=h, tq=tq, k=k, msl=msl):
                    act(PT[:, k, :], msl, AF.Square, [R_mix[h][tq]], [R_PT[k]])
                    mm(pb[k][:], onesb[:], PT[:, k, :], True, True, [R_c, R_PT[k]], [R_pb[k]])

                def s2(h=h, tq=tq, k=k, msl=msl):
                    act(tmpf[:, k, :], pb[k][:], AF.Ln, [R_pb[k], R_eps], [R_tmp[k]], bias=epsc[:], scale=1.0 / 128)
                    act(tmpf[:, k, :], tmpf[:, k, :], AF.Exp, [R_tmp[k]], [R_tmp[k]], scale=-0.5)
                    stt("dve", msl, msl, gcol[:, 0:1], tmpf[:, k, :], ALU.mult, ALU.mult,
                        [R_mix[h][tq], R_tmp[k], R_lam], [R_mix[h][tq]])
                    if h == 3:
                        out_proj(0, range(4 * tq, 4 * tq + 4))

                sl1.append(s1)
                sl2.append(s2)
        run_pipeline(sl1, sl2, 1)

    def dil_units(d, qt):
        u = []
        if d == 1:
            for m in range(max(0, 4 * qt - 1), 4 * qt + 4):
                j = m - 4 * qt
                if j == -1:
                    u.append((128 * m, 1, 512 * qt, 1, 0, 128, 128))
                elif j < 3:
                    u.append((128 * m, 1, 512 * qt + 128 * j, 1, 128 * j, 256, 0))
                else:
                    u.append((128 * m, 1, 512 * qt + 384, 1, 384, 128, 0))
        elif d == 4:
            r = qt
            for m in range(4):
                n = 256 if m < 3 else 128
                u.append((512 * m + r, 4, 4 * 128 * m + r, 4, 128 * m, n, 0))
        else:
            for rr in range(4):
                r = 4 * qt + rr
                u.append((r, 16, r, 16, 128 * rr, 128, 0))
        return u

    def dil_stage():
        wout_dma(512)
        SC.op("dve", lambda e: e.memset(vring[:], 1.0), [], R_vr)
        st_i = 0
        vr_i = 0
        nd_i = 0
        pt_i = 0
        for g in range(2):
            pj1, pj2, specs = [], [], []
            for pl in range(2):
                p = 2 * g + pl
                specs.append((1536 + 128 * p, QK[:, 2 * pl, :], R_QK[2 * pl], True))
                specs.append((2048 + 128 * p, QK[:, 2 * pl + 1, :], R_QK[2 * pl + 1], True))
            for pl in range(2):
                p = 2 * g + pl
                specs.append((2560 + 128 * p, gvT[:, pl, :], [R_VV] * 4, False))
            proj_group(specs, pj1, pj2)
            run_pipeline(pj1, pj2, 1)
            for pl in range(2):
                p = 2 * g + pl
                Qt, Kt = QK[:, 2 * pl, :], QK[:, 2 * pl + 1, :]
                st1, st2 = [], []
                for bi, d in enumerate((1, 4, 16)):
                    for qt in range(4):
                        ndk = nd_i % 2
                        nd_i += 1
                        units = dil_units(d, qt)
                        groups, cur, used = [], [], 0
                        for ui, unit in enumerate(units):
                            vs = vr_i % 10
                            vr_i += 1
                            n = unit[5]
                            if used + n > 512:
                                groups.append(cur)
                                cur, used = [], 0
                            cur.append((ui, unit, vs, used))
                            used += n
                        groups.append(cur)
                        tile_units = [(unit, vs) for grp in groups for (ui, unit, vs, off) in grp]
                        kp0 = st_i % 2
                        st_i += 1

                        def pro(pl=pl, tile_units=tile_units, kp=kp0):
                            tb = pb[2 * kp].bitcast(BF16)
                            for i_, (unit, vs) in enumerate(tile_units):
                                k0, kst = unit[0], unit[1]
                                ksl = slice(k0, k0 + 127 * kst + 1, kst)
                                SC.op("pe", lambda e, o=tb[:, i_ * 128:(i_ + 1) * 128], i=gvT[:, pl, ksl]: e.transpose(o, i, ident[:]),
                                      [R_VV, R_ident], [R_pb[2 * kp]])
                            i0 = 0
                            while i0 < len(tile_units):
                                v0 = tile_units[i0][1]
                                cnt = 1
                                while i0 + cnt < len(tile_units) and tile_units[i0 + cnt][1] == v0 + cnt:
                                    cnt += 1
                                SC.op("act", lambda e, o=vring[:, v0:v0 + cnt, :].rearrange("p s (a b) -> p s a b", a=3)[:, :, 0:3:2, :],
                                      i=tb[:, i0 * 128:(i0 + cnt) * 128].rearrange("p (s a b) -> p s a b", s=cnt, a=2):
                                      e.copy(out=o, in_=i), [R_pb[2 * kp]], [R_vr[v] for v in range(v0, v0 + cnt)])
                                i0 += cnt

                        st1.append(pro)
                        st2.append(lambda: None)
                        for gi, grp in enumerate(groups):
                            kp = st_i % 2
                            st_i += 1
                            pp = pt_i % 3
                            pt_i += 1
                            tot = sum(u_[1][5] for u_ in grp)
                            last_grp = gi == len(groups) - 1
                            if pp < 2:
                                PTp = PT[:, 2 * pp:2 * pp + 2, :]
                                R_PTp = [R_PT[2 * pp], R_PT[2 * pp + 1]]
                            else:
                                PTp = PTx.rearrange("p (s t) -> p s t", s=2)
                                R_PTp = [R_PTx]

                            def s1(grp=grp, kp=kp, tot=tot, PTp=PTp, R_PTp=R_PTp, Qt=Qt, Kt=Kt, pl=pl):
                                for (ui, unit, vs, off) in grp:
                                    k0, kst, q0, qst, pc0, n, mc0 = unit
                                    ksl = slice(k0, k0 + 127 * kst + 1, kst)
                                    qsl = slice(q0, q0 + (n - 1) * qst + 1, qst)
                                    for hl in range(2):
                                        ps = slice(64 * hl, 64 * hl + 64)
                                        mm(pb[2 * kp + hl][:, off:off + n], Kt[ps, ksl], Qt[ps, qsl], True, True,
                                           R_QK[2 * pl + 1] + R_QK[2 * pl], [R_pb[2 * kp + hl]])
                                act(PTp[:, :, 0:tot], pall[:, 2 * kp:2 * kp + 2, 0:tot], AF.Exp,
                                    [R_pb[2 * kp], R_pb[2 * kp + 1]], R_PTp, scale=0.125)
                                for si, (ui, unit, vs, off) in enumerate(grp):
                                    k0, kst, q0, qst, pc0, n, mc0 = unit
                                    view = PTp[:, :, off:off + n]
                                    tt("dve", view, view,
                                       maskb[:, mc0:mc0 + n].unsqueeze(1).to_broadcast([128, 2, n]), ALU.mult, R_PTp + [R_c], R_PTp)

                            def s2(p=p, d=d, qt=qt, bi=bi, ndk=ndk, grp=grp, PTp=PTp, R_PTp=R_PTp, last_grp=last_grp):
                                hb_ = (pb[4 + 2 * ndk], pb[5 + 2 * ndk])
                                R_hb_ = (R_pb[4 + 2 * ndk], R_pb[5 + 2 * ndk])
                                for (ui, unit, vs, off) in grp:
                                    k0, kst, q0, qst, pc0, n, mc0 = unit
                                    for hl in range(2):
                                        SC.op("pe", lambda e, o=hb_[hl][:, pc0:pc0 + n], l=vring[:, vs, hl * 64:hl * 64 + 128],
                                              r=PTp[:, hl, off:off + n], s_=(ui == 0):
                                              e.matmul(o, lhsT=l, rhs=r, start=s_, stop=False, skip_group_check=True),
                                              [R_vr[vs]] + R_PTp, [R_hb_[hl]])
                                if not last_grp:
                                    return
                                for hl in range(2):
                                    if d == 1:
                                        dst = acc[:, hl, 512 * qt:512 * (qt + 1)]
                                        src = hb_[hl][:]
                                    elif d == 4:
                                        dst = acc[:, hl, qt:S:4]
                                        src = hb_[hl][:]
                                    else:
                                        dst = acc[:, hl, :].rearrange("p (i r) -> p r i", r=16)[:, 4 * qt:4 * qt + 4, :]
                                        src = hb_[hl][:].rearrange("p (r i) -> p r i", r=4)
                                    if bi == 0:
                                        SC.op("act", lambda e, o=dst, i=src: e.copy(out=o, in_=i), [R_hb_[hl]], [R_acc[hl]])
                                    else:
                                        tt("dve", dst, src, dst, ALU.add, [R_hb_[hl], R_acc[hl]], [R_acc[hl]])
                                if bi == 2 and qt == 3:
                                    for hl in range(2):
                                        nr = slice(64 * hl, 64 * hl + 64)
                                        dr = slice(64 - 64 * hl, 128 - 64 * hl)
                                        rd = tmpf[nr, 0:4, :].rearrange("p a b -> p (a b)")
                                        act(acc[dr, hl, :], acc[dr, hl, :], AF.Ln, [R_acc[hl]], [R_acc[hl]])
                                        act(acc[dr, hl, :], acc[dr, hl, :], AF.Exp, [R_acc[hl]], [R_acc[hl]], scale=-1.0)
                                        tcopy("dve", rd, acc[dr, hl, :], [R_acc[hl]], R_tmp)
                                        for tq in range(4):
                                            sl = slice(tq * 512, (tq + 1) * 512)
                                            tt("dve", mixT[nr, p, sl], acc[nr, hl, sl], rd[:, sl], ALU.mult,
                                               [R_acc[hl]] + R_tmp, [R_mix[p][tq]])

                            st1.append(s1)
                            st2.append(s2)
                run_pipeline(st1, st2, 2)

    def attention(s_, which):
        SC.barrier()
        if not rope_done[0]:
            for op_ in rope_table_ops(s_):
                op_()
        rope_done[0] = False
        if "A" in which:
            diff_stage()
        SC.barrier()
        if "B" in which:
            dil_stage()
            out_proj(512)
        if "ffn2" in stages:
            norm_to_hT(g_ffn2)
        SC.barrier()

    def final_norm(s):
        load_gain(g_fin)
        row_stats(0)
        for t in range(NT):
            k = t % 4
            if t % 4 == 0 and t // 4 + 1 < 4:
                row_stats(t // 4 + 1)
            stt("dve", ostage[:, k, :], x_sb[:, t, :], rstd[:, t:t + 1], gain[:], ALU.mult, ALU.mult,
                [R_x[t], R_rstd[t // 4], R_gain], [R_ostage[k]])
            out_dmas.append(dma("sp", y_out[s, t * 128:(t + 1) * 128, :], ostage[:, k, :], [R_ostage[k]], [],
                                ("out", k)))

    for s in range(nseq):
        if s > 0:
            load_x(s)
        if s == 0 and "ffn1" not in stages:
            load_constants()
            consts_done[0] = True
        if "ffn1" in stages:
            norm_to_hT(g_ffn1)
            if s == 0:
                load_constants()
                consts_done[0] = True
            with_attn = any(st in stages for st in ("attn", "attnA", "attnB"))
            ffn(w1g, w1u, w1d, extra=rope_table_ops(s) if with_attn else None)
            rope_done[0] = with_attn
        if any(st in stages for st in ("attn", "attnA", "attnB")):
            norm_to_hT(g_mix)
        if "attn" in stages:
            attention(s, "AB")
        elif "attnA" in stages:
            attention(s, "A")
        elif "attnB" in stages:
            attention(s, "B")
        if "ffn2" in stages:
            if not any(st in stages for st in ("attn", "attnA", "attnB")):
                norm_to_hT(g_ffn2)
            ffn(w2g, w2u, w2d)
        final_norm(s)

    sem_eng = {e: es.enter_context(nc.semaphore(f"s_{e}")) for e in Sched.ENGS}
    sem_dma = {}
    for i, k in enumerate(SC.dma_keys):
        sem_dma[k] = es.enter_context(nc.semaphore(f"d_{i}"))
    SC.emit(nc, None, sem_eng, sem_dma)
    finals = {}
    for o in out_dmas:
        sem, val = o.signal
        finals[id(sem)] = (sem, max(val, finals.get(id(sem), (sem, 0))[1]))
    block = es.enter_context(nc.Block())

    @block.sync
    def _(e):
        SC.run_engine("sp", e, final_waits=list(finals.values()))

    @block.gpsimd
    def _(e):
        SC.run_engine("pool", e)

    @block.scalar
    def _(e):
        SC.run_engine("act", e)

    @block.vector
    def _(e):
        SC.run_engine("dve", e)

    @block.tensor
    def _(e):
        SC.run_engine("pe", e)

    es.close()
    return nc


def make_consts():
    perm = np.zeros((128, 128), np.float32)
    invf = np.zeros((128, 1), np.float32)
    inv = np.exp(-math.log(500000.0) * np.arange(8, dtype=np.float32) * 2.0 / 16).astype(np.float32)
    for base in (0, 64):
        for j in range(8):
            perm[base + j + 8, base + j] = -1.0
            perm[base + j, base + j + 8] = 1.0
            invf[base + j, 0] = inv[j]
            invf[base + j + 8, 0] = inv[j]
    kk = np.arange(128)[:, None]
    qq = np.arange(128)[None, :]
    mask = np.concatenate([(qq >= kk), (qq <= kk)], axis=1).astype(np.float32)
    return {
        "c_ident": np.eye(128, dtype=np.float32).astype(ml_dtypes.bfloat16),
        "c_perm": perm.astype(ml_dtypes.bfloat16),
        "c_mask": mask.astype(ml_dtypes.bfloat16),
        "c_invf": invf,
    }


def make_in_maps(inputs, nseq=2, ncores=NCORES):
    consts = make_consts()
    w = {}
    for k, v in inputs.items():
        if k in ("x", "positions"):
            continue
        a = np.asarray(v)
        if k == "final_norm":
            a = a.reshape(1, D)
        else:
            a = a.reshape(a.shape[1:]) if a.ndim == 3 else a
        w[k] = np.ascontiguousarray(a)
    x = np.asarray(inputs["x"])
    pos = np.asarray(inputs["positions"]).astype(np.int32)
    maps = []
    for c in range(ncores):
        m = dict(w)
        m.update(consts)
        m["x"] = np.ascontiguousarray(x[c * nseq:(c + 1) * nseq])
        m["positions"] = np.ascontiguousarray(pos[c * nseq:(c + 1) * nseq])
        maps.append(m)
    return maps


def kernel(**inputs):
    nc = build_nc(nseq=2)
    maps = make_in_maps(inputs, nseq=2, ncores=NCORES)
    res = run_bass_kernel_spmd(nc, maps, core_ids=list(range(NCORES)))
    return np.concatenate([np.asarray(r["y"]) for r in res.results], axis=0).astype(np.float32)
```
